# Optimizing a Trainium2 kernel written in Bass

```python
import math
import jax
import jax.numpy as jnp
from jax import lax
import numpy as np

D_MODEL = 4096
BATCH = 4
SEQ = 2048
DEPTH = 2
DEC_BATCH = 8
DEC_SEQ = 1
PAST_LEN = 16384
PAGE_SIZE = 128

N_EVEN = (DEPTH + 1) // 2
N_ODD = DEPTH // 2
HEAD_DIM = 128
A_HEADS = D_MODEL // (2 * HEAD_DIM)
D_A = A_HEADS * HEAD_DIM
A_WINDOWS = (128, 512, 2048)
A_DILATIONS = (1, 4, 16)
N_DIL = len(A_WINDOWS)
ROPE_THETA = 10000.0
D_B = D_MODEL - D_A
SSM_GROUP = 16
SSM_GROUPS = D_B // SSM_GROUP
SSM_STATE = 64
D_IN_EVEN = N_DIL * 3 * D_A + D_B
D_CONV = D_MODEL
CONV_WIDTH = 31
D_FF = 11008
FFN_CONV_WIDTH = 3
RMS_EPS = 1e-6
LN_EPS = 1e-5

kernel_name = 'hybrid_dilated_s5_conformer_decode_step'


def _a_buf_lens():
    return tuple(min(w, PAST_LEN) for w in A_WINDOWS)


def _rmsnorm(x, g):
    xf = x.astype(jnp.float32)
    y = xf * lax.rsqrt(jnp.mean(xf * xf, axis=-1, keepdims=True) + RMS_EPS)
    return (y * g.astype(jnp.float32)).astype(x.dtype)


def _layernorm(x, g, b):
    xf = x.astype(jnp.float32)
    mu = jnp.mean(xf, axis=-1, keepdims=True)
    var = jnp.mean(jnp.square(xf - mu), axis=-1, keepdims=True)
    y = (xf - mu) * lax.rsqrt(var + LN_EPS) * g.astype(jnp.float32) + b.astype(jnp.float32)
    return y.astype(x.dtype)


def _rope(x, pos):
    half = HEAD_DIM // 2
    inv_freq = ROPE_THETA ** (-jnp.arange(half, dtype=jnp.float32) / half)
    ang = pos.astype(jnp.float32)[:, None] * inv_freq[None, :]
    cos = jnp.cos(ang)[None, :, None, :]
    sin = jnp.sin(ang)[None, :, None, :]
    xf = x.astype(jnp.float32)
    x1, x2 = xf[..., :half], xf[..., half:]
    return jnp.concatenate([x1 * cos - x2 * sin, x1 * sin + x2 * cos], axis=-1).astype(x.dtype)


def _dilated_band_attention(q, k, v, dilation, steps):
    n, s, h, hd = q.shape
    sub = s // dilation
    bd = n * dilation

    def to_sub(t):
        return t.reshape(n, sub, dilation, h, hd).transpose(0, 2, 1, 3, 4).reshape(bd, sub, h, hd)

    qs, ks, vs = to_sub(q), to_sub(k), to_sub(v)
    nb = -(-sub // steps)
    pad_r = nb * steps - sub
    qs = jnp.pad(qs, ((0, 0), (0, pad_r), (0, 0), (0, 0)))
    kpad = ((0, 0), (steps, pad_r), (0, 0), (0, 0))
    kb = jnp.pad(ks, kpad).reshape(bd, nb + 1, steps, h, hd)
    vb = jnp.pad(vs, kpad).reshape(bd, nb + 1, steps, h, hd)
    k_win = jnp.concatenate([kb[:, :-1], kb[:, 1:]], axis=2)
    v_win = jnp.concatenate([vb[:, :-1], vb[:, 1:]], axis=2)
    q_blk = qs.reshape(bd, nb, steps, h, hd)
    scores = jnp.einsum('bnqhd,bnkhd->bnhqk', q_blk, k_win,
                        preferred_element_type=jnp.float32) * (hd ** -0.5)
    qi = jnp.arange(steps)[:, None]
    ki = jnp.arange(2 * steps)[None, :]
    key_sub = jnp.arange(nb)[:, None, None] * steps + ki[None] - steps
    valid = (ki >= qi)[None] & (ki <= qi + steps)[None] & (key_sub >= 0)
    scores = jnp.where(valid[None, :, None], scores, -jnp.inf)
    m = jnp.max(scores, axis=-1, keepdims=True)
    p = jnp.exp(scores - m)
    l = jnp.sum(p, axis=-1, keepdims=True)
    o = jnp.einsum('bnhqk,bnkhd->bnqhd', p.astype(v.dtype), v_win,
                   preferred_element_type=jnp.float32)
    o = o / jnp.swapaxes(l, 2, 3)
    lse = jnp.swapaxes((m + jnp.log(l))[..., 0], 2, 3)
    o = o.reshape(bd, nb * steps, h, hd)[:, :sub]
    o = o.reshape(n, dilation, sub, h, hd).transpose(0, 2, 1, 3, 4).reshape(n, s, h, hd)
    lse = lse.reshape(bd, nb * steps, h)[:, :sub]
    lse = lse.reshape(n, dilation, sub, h).transpose(0, 2, 1, 3).reshape(n, s, h)
    return o, lse


def _dilated_step_attention(q, k, v, kv_buf, dilation, window):
    n, t, h, hd = q.shape
    buf_len = kv_buf.shape[1]
    k_all = jnp.concatenate([kv_buf[:, :, 0].astype(k.dtype), k], axis=1)
    v_all = jnp.concatenate([kv_buf[:, :, 1].astype(v.dtype), v], axis=1)
    offs = jnp.arange(window // dilation + 1) * dilation
    rows = (buf_len + jnp.arange(t))[:, None] - offs[None, :]
    valid = rows >= 0
    rows = jnp.maximum(rows, 0)
    k_g = jnp.take(k_all, rows, axis=1)
    v_g = jnp.take(v_all, rows, axis=1)
    scores = jnp.einsum('bthd,btjhd->bthj', q, k_g,
                        preferred_element_type=jnp.float32) * (hd ** -0.5)
    scores = jnp.where(valid[None, :, None, :], scores, -jnp.inf)
    m = jnp.max(scores, axis=-1, keepdims=True)
    p = jnp.exp(scores - m)
    l = jnp.sum(p, axis=-1, keepdims=True)
    o = jnp.einsum('bthj,btjhd->bthd', p.astype(v.dtype), v_g,
                   preferred_element_type=jnp.float32) / l
    lse = (m + jnp.log(l))[..., 0]
    return o, lse


def _last_rows(prev, new, length):
    ext = new if prev is None else jnp.concatenate([prev.astype(new.dtype), new], axis=1)
    r = ext.shape[1]
    if r >= length:
        return ext[:, r - length:]
    pad = [(0, 0)] * ext.ndim
    pad[1] = (length - r, 0)
    return jnp.pad(ext, pad)


def _ssm_combine(e1, e2):
    ar1, ai1, br1, bi1 = e1
    ar2, ai2, br2, bi2 = e2
    return (ar2 * ar1 - ai2 * ai1,
            ar2 * ai1 + ai2 * ar1,
            ar2 * br1 - ai2 * bi1 + br2,
            ar2 * bi1 + ai2 * br1 + bi2)


def _s5(u, a_re, a_im, b_re, b_im, c_re, c_im, d_skip, log_dt, h0):
    f32 = jnp.float32
    lr, li = a_re.astype(f32), a_im.astype(f32)
    dt = jnp.exp(log_dt.astype(f32))[:, None]
    mag = jnp.exp(lr * dt)
    ab_re, ab_im = mag * jnp.cos(li * dt), mag * jnp.sin(li * dt)
    den = lr * lr + li * li
    z_re = ((ab_re - 1.0) * lr + ab_im * li) / den
    z_im = (ab_im * lr - (ab_re - 1.0) * li) / den
    br, bi = b_re.astype(f32), b_im.astype(f32)
    bb_re = z_re[..., None] * br - z_im[..., None] * bi
    bb_im = z_re[..., None] * bi + z_im[..., None] * br
    uf = u.astype(f32)
    bu_re = jnp.einsum('ntgc,gpc->ntgp', uf, bb_re)
    bu_im = jnp.einsum('ntgc,gpc->ntgp', uf, bb_im)
    h0r, h0i = h0[..., 0].astype(f32), h0[..., 1].astype(f32)
    bu_re = bu_re.at[:, 0].add(ab_re * h0r - ab_im * h0i)
    bu_im = bu_im.at[:, 0].add(ab_re * h0i + ab_im * h0r)
    a_r = jnp.broadcast_to(ab_re, bu_re.shape)
    a_i = jnp.broadcast_to(ab_im, bu_re.shape)
    _, _, h_re, h_im = lax.associative_scan(_ssm_combine, (a_r, a_i, bu_re, bu_im), axis=1)
    y = (jnp.einsum('ntgp,gcp->ntgc', h_re, c_re.astype(f32))
         - jnp.einsum('ntgp,gcp->ntgc', h_im, c_im.astype(f32))
         + d_skip.astype(f32) * uf)
    h_last = jnp.stack([h_re[:, -1], h_im[:, -1]], axis=-1)
    return y, h_last


def _causal_depthwise(ext, w, t):
    w = w.astype(ext.dtype)
    out = ext[:, 0:t] * w[0]
    for j in range(1, w.shape[0]):
        out = out + ext[:, j:j + t] * w[j]
    return out


def _even_mixer(hn, pos, kv_bufs, h0, w_in, w_out, a_re, a_im, b_re, b_im, c_re, c_im,
                d_skip, log_dt, w_glu, b_glu):
    n, t, _ = hn.shape
    z = hn @ w_in
    n_qkv = N_DIL * 3 * D_A
    qkv = z[..., :n_qkv].reshape(n, t, N_DIL, 3, A_HEADS, HEAD_DIM)
    u = z[..., n_qkv:].reshape(n, t, SSM_GROUPS, SSM_GROUP)
    buf_lens = _a_buf_lens()
    outs, lses, new_kv = [], [], []
    for g in range(N_DIL):
        q = _rope(qkv[:, :, g, 0], pos)
        k = _rope(qkv[:, :, g, 1], pos)
        v = qkv[:, :, g, 2]
        if kv_bufs is None:
            o, lse = _dilated_band_attention(q, k, v, A_DILATIONS[g], A_WINDOWS[g] // A_DILATIONS[g])
            prev = None
        else:
            o, lse = _dilated_step_attention(q, k, v, kv_bufs[g], A_DILATIONS[g], A_WINDOWS[g])
            prev = kv_bufs[g]
        new_kv.append(_last_rows(prev, jnp.stack([k, v], axis=2), buf_lens[g]))
        outs.append(o)
        lses.append(lse)
    mix_w = jax.nn.softmax(jnp.stack(lses), axis=0)
    attn = jnp.einsum('gnth,gnthd->nthd', mix_w, jnp.stack(outs)).reshape(n, t, D_A)
    if h0 is None:
        h0 = jnp.zeros((n, SSM_GROUPS, SSM_STATE, 2), jnp.float32)
    y, h_new = _s5(u, a_re, a_im, b_re, b_im, c_re, c_im, d_skip, log_dt, h0)
    y = jax.nn.gelu(y.reshape(n, t, D_B))
    ssm_out = y * jax.nn.sigmoid(y @ w_glu.astype(jnp.float32) + b_glu.astype(jnp.float32))
    cat = jnp.concatenate([attn, ssm_out], axis=-1).astype(hn.dtype)
    return cat @ w_out, new_kv, h_new


def _conv_module(hn, buf, w_pw1, b_pw1, w_dw, b_dw, ln_g, ln_b, w_pw2, b_pw2):
    n, t, _ = hn.shape
    z = hn @ w_pw1 + b_pw1
    a, gate = jnp.split(z, 2, axis=-1)
    u = a * jax.nn.sigmoid(gate)
    if buf is None:
        buf = jnp.zeros((n, CONV_WIDTH - 1, D_CONV), u.dtype)
    ext = jnp.concatenate([buf.astype(u.dtype), u], axis=1)
    y = _causal_depthwise(ext, w_dw, t) + b_dw
    y = _layernorm(y, ln_g, ln_b)
    y = y * jax.nn.sigmoid(y)
    return y @ w_pw2 + b_pw2, ext[:, -(CONV_WIDTH - 1):]


def _conv_ffn(hn, buf, w_up, w_dw, b_dw, w_down):
    n, t, _ = hn.shape
    z = hn @ w_up
    if buf is None:
        buf = jnp.zeros((n, FFN_CONV_WIDTH - 1, 2 * D_FF), z.dtype)
    ext = jnp.concatenate([buf.astype(z.dtype), z], axis=1)
    zc = _causal_depthwise(ext, w_dw, t) + b_dw
    g, v = jnp.split(zc, 2, axis=-1)
    return (jax.nn.silu(g) * v) @ w_down, ext[:, -(FFN_CONV_WIDTH - 1):]


def _trunk(x, pos, st, p):
    new_a = [[] for _ in range(N_DIL)]
    new_ssm, new_conv, new_ffn = [], [], []
    for i in range(DEPTH):
        j = i // 2
        hn = _rmsnorm(x, p['norm_mix'][i])
        if i % 2 == 0:
            kv_bufs = None if st is None else [st['a'][g][j] for g in range(N_DIL)]
            h0 = None if st is None else st['ssm'][j]
            mix, kvs, h_new = _even_mixer(
                hn, pos, kv_bufs, h0, p['w_in_even'][j], p['w_out_even'][j],
                p['ssm_a_re'][j], p['ssm_a_im'][j], p['ssm_b_re'][j], p['ssm_b_im'][j],
                p['ssm_c_re'][j], p['ssm_c_im'][j], p['ssm_d'][j], p['ssm_log_dt'][j],
                p['w_glu'][j], p['b_glu'][j])
            for g in range(N_DIL):
                new_a[g].append(kvs[g])
            new_ssm.append(h_new)
        else:
            cb = None if st is None else st['conv'][j]
            mix, c_new = _conv_module(
                hn, cb, p['w_pw1'][j], p['b_pw1'][j], p['w_dw'][j], p['b_dw'][j],
                p['ln_g'][j], p['ln_b'][j], p['w_pw2'][j], p['b_pw2'][j])
            new_conv.append(c_new)
        x = x + mix.astype(x.dtype)
        fb = None if st is None else st['ffn'][i]
        f, f_new = _conv_ffn(_rmsnorm(x, p['norm_ffn'][i]), fb, p['w_up'][i],
                             p['ffn_dw'][i], p['ffn_dw_b'][i], p['w_down'][i])
        new_ffn.append(f_new)
        x = x + f.astype(x.dtype)
    y = _rmsnorm(x, p['norm_final'])
    return (y, [jnp.stack(a) for a in new_a], jnp.stack(new_ssm),
            jnp.stack(new_conv), jnp.stack(new_ffn))


def setup_inputs(seed: int = 0) -> dict:
    key = jax.random.key(seed)
    keys = iter(jax.random.split(key, 40))
    f32 = jnp.float32

    def nrm(shape, scale):
        return jax.random.normal(next(keys), shape, f32) * scale

    buf = _a_buf_lens()
    x_prompt = nrm((BATCH, SEQ, D_MODEL), 1.0)
    x_sample = nrm((DEC_BATCH, DEC_SEQ, D_MODEL), 1.0)
    cache_a0 = nrm((N_EVEN, DEC_BATCH, buf[0], 2, A_HEADS, HEAD_DIM), 1.0)
    cache_a1 = nrm((N_EVEN, DEC_BATCH, buf[1], 2, A_HEADS, HEAD_DIM), 1.0)
    cache_a2 = nrm((N_EVEN, DEC_BATCH, buf[2], 2, A_HEADS, HEAD_DIM), 1.0)
    state_ssm = nrm((N_EVEN, DEC_BATCH, SSM_GROUPS, SSM_STATE, 2), 0.1)
    state_conv = nrm((N_ODD, DEC_BATCH, CONV_WIDTH - 1, D_CONV), 0.5)
    state_ffn = nrm((DEPTH, DEC_BATCH, FFN_CONV_WIDTH - 1, 2 * D_FF), 1.0)
    norm_mix = 1.0 + nrm((DEPTH, D_MODEL), 0.02)
    norm_ffn = 1.0 + nrm((DEPTH, D_MODEL), 0.02)
    norm_final = 1.0 + nrm((D_MODEL,), 0.02)
    w_in_even = nrm((N_EVEN, D_MODEL, D_IN_EVEN), D_MODEL ** -0.5)
    w_out_even = nrm((N_EVEN, D_A + D_B, D_MODEL), (D_A + D_B) ** -0.5)
    ssm_a_re = -0.5 + nrm((N_EVEN, SSM_GROUPS, SSM_STATE), 0.01)
    ssm_a_im = jnp.pi * jnp.arange(SSM_STATE, dtype=f32) + nrm((N_EVEN, SSM_GROUPS, SSM_STATE), 0.01)
    ssm_b_re = nrm((N_EVEN, SSM_GROUPS, SSM_STATE, SSM_GROUP), (2 * SSM_GROUP) ** -0.5)
    ssm_b_im = nrm((N_EVEN, SSM_GROUPS, SSM_STATE, SSM_GROUP), (2 * SSM_GROUP) ** -0.5)
    ssm_c_re = nrm((N_EVEN, SSM_GROUPS, SSM_GROUP, SSM_STATE), SSM_STATE ** -0.5)
    ssm_c_im = nrm((N_EVEN, SSM_GROUPS, SSM_GROUP, SSM_STATE), SSM_STATE ** -0.5)
    ssm_d = nrm((N_EVEN, SSM_GROUPS, SSM_GROUP), 1.0)
    ssm_log_dt = jax.random.uniform(next(keys), (N_EVEN, SSM_GROUPS), f32,
                                    minval=math.log(1e-3), maxval=math.log(1e-1))
    w_glu = nrm((N_EVEN, D_B, D_B), D_B ** -0.5)
    b_glu = nrm((N_EVEN, D_B), 0.02)
    w_pw1 = nrm((N_ODD, D_MODEL, 2 * D_CONV), D_MODEL ** -0.5)
    b_pw1 = nrm((N_ODD, 2 * D_CONV), 0.02)
    w_dw = nrm((N_ODD, CONV_WIDTH, D_CONV), CONV_WIDTH ** -0.5)
    b_dw = nrm((N_ODD, D_CONV), 0.02)
    ln_g = 1.0 + nrm((N_ODD, D_CONV), 0.02)
    ln_b = nrm((N_ODD, D_CONV), 0.02)
    w_pw2 = nrm((N_ODD, D_CONV, D_MODEL), D_CONV ** -0.5)
    b_pw2 = nrm((N_ODD, D_MODEL), 0.02)
    w_up = nrm((DEPTH, D_MODEL, 2 * D_FF), D_MODEL ** -0.5)
    ffn_dw = nrm((DEPTH, FFN_CONV_WIDTH, 2 * D_FF), FFN_CONV_WIDTH ** -0.5)
    ffn_dw_b = nrm((DEPTH, 2 * D_FF), 0.02)
    w_down = nrm((DEPTH, D_FF, D_MODEL), D_FF ** -0.5)
    return {
        'x_prompt': x_prompt, 'x_sample': x_sample,
        'cache_a0': cache_a0, 'cache_a1': cache_a1, 'cache_a2': cache_a2,
        'state_ssm': state_ssm, 'state_conv': state_conv, 'state_ffn': state_ffn,
        'norm_mix': norm_mix, 'norm_ffn': norm_ffn, 'norm_final': norm_final,
        'w_in_even': w_in_even, 'w_out_even': w_out_even,
        'ssm_a_re': ssm_a_re, 'ssm_a_im': ssm_a_im, 'ssm_b_re': ssm_b_re, 'ssm_b_im': ssm_b_im,
        'ssm_c_re': ssm_c_re, 'ssm_c_im': ssm_c_im, 'ssm_d': ssm_d, 'ssm_log_dt': ssm_log_dt,
        'w_glu': w_glu, 'b_glu': b_glu,
        'w_pw1': w_pw1, 'b_pw1': b_pw1, 'w_dw': w_dw, 'b_dw': b_dw,
        'ln_g': ln_g, 'ln_b': ln_b, 'w_pw2': w_pw2, 'b_pw2': b_pw2,
        'w_up': w_up, 'ffn_dw': ffn_dw, 'ffn_dw_b': ffn_dw_b, 'w_down': w_down,
    }


def reference(x_prompt, x_sample, cache_a0, cache_a1, cache_a2, state_ssm, state_conv, state_ffn,
              norm_mix, norm_ffn, norm_final, w_in_even, w_out_even,
              ssm_a_re, ssm_a_im, ssm_b_re, ssm_b_im, ssm_c_re, ssm_c_im, ssm_d, ssm_log_dt,
              w_glu, b_glu, w_pw1, b_pw1, w_dw, b_dw, ln_g, ln_b, w_pw2, b_pw2,
              w_up, ffn_dw, ffn_dw_b, w_down):
    p = {
        'norm_mix': norm_mix, 'norm_ffn': norm_ffn, 'norm_final': norm_final,
        'w_in_even': w_in_even, 'w_out_even': w_out_even,
        'ssm_a_re': ssm_a_re, 'ssm_a_im': ssm_a_im, 'ssm_b_re': ssm_b_re, 'ssm_b_im': ssm_b_im,
        'ssm_c_re': ssm_c_re, 'ssm_c_im': ssm_c_im, 'ssm_d': ssm_d, 'ssm_log_dt': ssm_log_dt,
        'w_glu': w_glu, 'b_glu': b_glu,
        'w_pw1': w_pw1, 'b_pw1': b_pw1, 'w_dw': w_dw, 'b_dw': b_dw,
        'ln_g': ln_g, 'ln_b': ln_b, 'w_pw2': w_pw2, 'b_pw2': b_pw2,
        'w_up': w_up, 'ffn_dw': ffn_dw, 'ffn_dw_b': ffn_dw_b, 'w_down': w_down,
    }
    pos_prompt = jnp.arange(SEQ, dtype=jnp.int32)
    pos_sample = PAST_LEN + jnp.arange(DEC_SEQ, dtype=jnp.int32)
    y_prompt, a_p, ssm_p, conv_p, ffn_p = _trunk(x_prompt, pos_prompt, None, p)
    st = {'a': (cache_a0, cache_a1, cache_a2), 'ssm': state_ssm,
          'conv': state_conv, 'ffn': state_ffn}
    y_sample, a_s, ssm_s, conv_s, ffn_s = _trunk(x_sample, pos_sample, st, p)
    return (y_prompt, y_sample, a_p[0], a_s[0], a_p[1], a_s[1], a_p[2], a_s[2],
            ssm_p, ssm_s, conv_p, conv_s, ffn_p, ffn_s)
```

```python
import math
from contextlib import ExitStack
import numpy as np
import concourse.bass as bass
import concourse.mybir as mybir
from concourse.bass_utils import run_bass_kernel_spmd

F32 = mybir.dt.float32
BF16 = mybir.dt.bfloat16
I32 = mybir.dt.int32
ALU = mybir.AluOpType
AF = mybir.ActivationFunctionType
AX = mybir.AxisListType

FULL = dict(DM=4096, DFF=11008, T=2048, BATCH=4, DEC=8, PAST=16384)
TWO_PI = 2.0 * math.pi


def derive(cfg):
    c = dict(cfg)
    c["NCH"] = c["DM"] // 128
    c["AH"] = c["DM"] // 256
    c["DA"] = c["AH"] * 128
    c["DB"] = c["DM"] - c["DA"]
    c["NCB"] = c["DB"] // 128
    c["SG"] = c["DB"] // 16
    c["NPT"] = c["SG"] // 2
    c["NQKV"] = 9 * c["AH"]
    c["NMIN"] = c["NQKV"] + c["NCB"]
    c["NFF"] = c["DFF"] // 128
    return c


class Buf:
    __slots__ = ("name", "last_w", "readers", "lsem", "ssem")

    def __init__(self, name):
        self.name = name
        self.last_w = None
        self.readers = {}
        self.lsem = None
        self.ssem = None


class Prog:
    ENGS = ("pe", "act", "dve", "pool", "sp")

    def __init__(self, nc, stack):
        self.nc = nc
        self.stack = stack
        self.sems = {}
        self.cnt = {}
        self.waited = {e: {} for e in self.ENGS}
        self.streams = {e: [] for e in self.ENGS}
        self.free_dsems = []
        self.used_dsems = []
        self.n_dsem = 0
        self.bufs = []
        for e in ("pe", "act", "dve", "pool"):
            self.sems[e] = stack.enter_context(nc.semaphore("c_" + e))
            self.cnt[e] = 0

    def buf(self, name):
        b = Buf(name)
        self.bufs.append(b)
        return b

    def _get_dsem(self, q):
        pref = "dq" if q == "pool" else "ds"
        cand = [k for k in self.free_dsems if k.startswith(pref)]
        if cand:
            k = cand[-1]
            self.free_dsems.remove(k)
        else:
            k = "%s%d" % (pref, self.n_dsem)
            self.n_dsem += 1
            self.sems[k] = self.stack.enter_context(self.nc.semaphore(k))
            self.cnt[k] = 0
        self.used_dsems.append(k)
        return k

    def _deps(self, eng, reads, writes, same_raw):
        toks = {}

        def add(tok, kind):
            if tok is None:
                return
            k, v = tok
            if k == eng and not same_raw:
                return
            if toks.get(k, 0) < v:
                toks[k] = v

        writes = list(writes) + [b for b in reads if b.name.startswith("!ps")]
        for b in reads:
            add(b.last_w, "raw")
        for b in writes:
            add(b.last_w, "waw")
            for k, v in b.readers.items():
                add((k, v), "war")
        out = []
        for k, v in toks.items():
            if self.waited[eng].get(k, 0) >= v:
                continue
            self.waited[eng][k] = v
            out.append((k, v))
        return out

    def _commit(self, tok, reads, writes):
        k, v = tok
        writes = list(writes) + [b for b in reads if b.name.startswith("!ps")]
        for b in writes:
            b.last_w = tok
            b.readers = {}
        for b in reads:
            if b.readers.get(k, 0) < v:
                b.readers[k] = v

    def op(self, eng, fn, reads=(), writes=(), same_raw=True):
        waits = self._deps(eng, reads, writes, same_raw)
        self.cnt[eng] += 1
        tok = (eng, self.cnt[eng])
        self.streams[eng].append((waits, fn, (eng, 1)))
        self._commit(tok, reads, writes)
        return tok

    def dma(self, q, fn, reads=(), writes=(), sembuf=None, store=False):
        waits = self._deps(q, reads, writes, True)
        if sembuf is None:
            sembuf = (reads[0] if store else writes[0])
        if store:
            if sembuf.ssem is None or not sembuf.ssem.startswith("dq" if q == "pool" else "ds"):
                sembuf.ssem = self._get_dsem(q)
            k = sembuf.ssem
        else:
            if sembuf.lsem is None or not sembuf.lsem.startswith("dq" if q == "pool" else "ds"):
                sembuf.lsem = self._get_dsem(q)
            k = sembuf.lsem
        self.cnt[k] += 16
        tok = (k, self.cnt[k])
        self.streams[q].append((waits, fn, (k, 16)))
        self._commit(tok, reads, writes)
        return tok

    def fence(self, q, tok):
        k, v = tok
        if self.waited[q].get(k, 0) < v:
            self.waited[q][k] = v
            self.streams[q].append(([(k, v)], None, None))

    def barrier(self):
        allk = [(e, self.cnt[e]) for e in ("pe", "act", "dve", "pool") if self.cnt[e] > 0]
        allk += [(k, self.cnt[k]) for k in self.sems if k.startswith("d") and self.cnt[k] > 0]
        for e in self.ENGS:
            waits = []
            for k, v in allk:
                if k == e:
                    continue
                if self.waited[e].get(k, 0) >= v:
                    continue
                self.waited[e][k] = v
                waits.append((k, v))
            if waits:
                self.streams[e].append((waits, None, None))
        for b in self.bufs:
            b.last_w = None
            b.readers = {}
            b.lsem = None
            b.ssem = None
        self.free_dsems = sorted(set(self.free_dsems) | set(self.used_dsems))
        self.used_dsems = []

    def emit(self):
        nc = self.nc
        sems = self.sems
        streams = self.streams

        def run(e, engobj):
            for waits, fn, inc in streams[e]:
                for k, v in waits:
                    engobj.wait_ge(sems[k], v)
                if fn is not None:
                    ins = fn(engobj)
                    ins.then_inc(sems[inc[0]], inc[1])

        with nc.Block() as block:
            @block.tensor
            def _(e):
                run("pe", e)

            @block.scalar
            def _(e):
                run("act", e)

            @block.vector
            def _(e):
                run("dve", e)

            @block.gpsimd
            def _(e):
                run("pool", e)

            @block.sync
            def _(e):
                run("sp", e)


class Arena:
    def __init__(self, nc, st, nbytes):
        self.t = st.enter_context(nc.sbuf_tensor("arena", [128, nbytes // 4], F32))
        self.nbytes = nbytes
        self.off = 0
        self.base = 0

    def freeze(self):
        self.base = self.off

    def reset(self):
        self.off = self.base

    def alloc(self, shape, dt=F32):
        n = int(np.prod(shape))
        esz = 2 if dt == BF16 else 4
        sz = (n * esz + 63) // 64 * 64
        a = self.off
        self.off += sz
        assert self.off <= self.nbytes, ("SBUF arena overflow", self.off, self.nbytes)
        v = self.t[:, a // 4:(a + sz) // 4]
        if dt != F32:
            v = v.bitcast(dt)
        v = v[:, 0:n]
        if len(shape) == 2:
            v = v.rearrange("p (a b) -> p a b", a=shape[0])
        elif len(shape) == 3:
            v = v.rearrange("p (a b c) -> p a b c", a=shape[0], b=shape[1])
        elif len(shape) == 4:
            v = v.rearrange("p (a b c d) -> p a b c d", a=shape[0], b=shape[1], c=shape[2])
        return v


def bcast_free(ap2, n):
    return bass.AP(ap2.tensor, ap2.offset, [list(ap2.ap[0]), [0, n]])


class KB:
    def __init__(self, cfg):
        self.c = derive(cfg)
        self.nc = bass.Bass("TRN2", target_bir_lowering=False)
        self.din = {}
        self.dout = {}
        self.scr = {}

    def inp(self, name, shape):
        self.din[name] = self.nc.dram_tensor(name, list(shape), F32, kind="ExternalInput").ap()
        return self.din[name]

    def outp(self, name, shape):
        self.dout[name] = self.nc.dram_tensor(name, list(shape), F32, kind="ExternalOutput").ap()
        return self.dout[name]

    def scratch(self, name, shape, dt=F32):
        self.scr[name] = self.nc.dram_tensor(name, list(shape), dt).ap()
        return self.scr[name]

    def A(self, eng, fn, reads, writes, same_raw=True):
        return self.P.op(eng, fn, reads, writes, same_raw)

    def mm(self, out, lhsT, rhs, start, stop, reads, writes):
        return self.P.op("pe", lambda e: e.matmul(out, lhsT=lhsT, rhs=rhs, start=start, stop=stop),
                         reads, writes, same_raw=False)

    def tr(self, out, in_, ident, reads, writes):
        return self.P.op("pe", lambda e: e.transpose(out, in_, ident), reads, writes, same_raw=False)

    def ld(self, out, in_, buf, q="sp"):
        return self.P.dma(q, lambda e: e.dma_start(out=out, in_=in_), reads=(), writes=(buf,))

    def stq(self, out, in_, buf, q="sp"):
        return self.P.dma(q, lambda e: e.dma_start(out=out, in_=in_), reads=(buf,), writes=(), store=True)

    def tt(self, eng, out, in0, in1, op, reads, writes):
        return self.P.op(eng, lambda e: e.tensor_tensor(out=out, in0=in0, in1=in1, op=op), reads, writes)

    def ts(self, eng, out, in0, s1, s2, op0, op1, reads, writes):
        if s2 is None:
            return self.P.op(eng, lambda e: e.tensor_scalar(out=out, in0=in0, scalar1=s1, scalar2=None, op0=op0),
                             reads, writes)
        return self.P.op(eng, lambda e: e.tensor_scalar(out=out, in0=in0, scalar1=s1, scalar2=s2, op0=op0, op1=op1),
                         reads, writes)

    def stt(self, eng, out, in0, scalar, in1, op0, op1, reads, writes):
        eng = "dve"
        return self.P.op(eng, lambda e: e.scalar_tensor_tensor(out=out, in0=in0, scalar=scalar, in1=in1,
                                                               op0=op0, op1=op1), reads, writes)

    def act(self, out, in_, func, reads, writes, bias=None, scale=1.0):
        if bias is None:
            return self.P.op("act", lambda e: e.activation(out=out, in_=in_, func=func, scale=scale),
                             reads, writes)
        return self.P.op("act", lambda e: e.activation(out=out, in_=in_, func=func, bias=bias, scale=scale),
                         reads, writes)

    def cp(self, eng, out, in_, reads, writes):
        if eng == "act":
            return self.P.op("act", lambda e: e.copy(out=out, in_=in_), reads, writes)
        return self.P.op(eng, lambda e: e.tensor_copy(out=out, in_=in_), reads, writes)

    def memset(self, eng, ap, val, writes):
        return self.P.op(eng, lambda e: e.memset(ap, val), (), writes)


ARENA_BYTES = 200 * 1024
DIL = (1, 4, 16)


def build_program(cfg):
    kb = KB(cfg)
    c = kb.c
    nc = kb.nc
    DM, T, NCH, AH, NCB, NPT, NMIN, NFF, NQKV = (c[k] for k in ("DM", "T", "NCH", "AH", "NCB", "NPT", "NMIN", "NFF", "NQKV"))
    SG = c["SG"]
    LG = (128, 512, 2048)
    NSB = T // 1024

    I = kb.inp
    xT = I("xT", [DM, T]); xs_d = I("xs", [128, NCH])
    cache = [I("cache%d" % g, [LG[g], 2 * AH * 128]) for g in range(3)]
    st_ssm = I("st_ssm", [128, NPT * 2]); st_conv = I("st_conv", [128, NCH * 30]); st_ffn = I("st_ffn", [128, 2 * 2 * NFF * 2])
    g_mix = I("g_mix", [128, 2 * NCH]); g_ffn = I("g_ffn", [128, 2 * NCH]); g_fin = I("g_fin", [128, NCH])
    w_in_t = I("w_in_t", [NMIN, 128, NCH * 128]); w_out_t = I("w_out_t", [NCH, 128, NCH * 128])
    w_glu_t = I("w_glu_t", [NCB, 128, NCB * 128]); b_glu = I("b_glu", [128, NCB])
    w_pw1_t = I("w_pw1_t", [2 * NCH, 128, NCH * 128]); b_pw1 = I("b_pw1", [128, 2 * NCH])
    w_dw = I("w_dw", [128, NCH * 31]); b_dw = I("b_dw", [128, NCH]); ln_g = I("ln_g", [128, NCH]); ln_b = I("ln_b", [128, NCH])
    w_pw2_t = I("w_pw2_t", [NCH, 128, NCH * 128]); b_pw2 = I("b_pw2", [128, NCH])
    w_up_t = I("w_up_t", [2, 2 * NFF, 128, NCH * 128]); w_down_t = I("w_down_t", [2, NCH, 128, NFF * 128])
    ffn_dw = I("ffn_dw", [128, 2 * 2 * NFF * 3]); ffn_b = I("ffn_b", [128, 2 * 2 * NFF])
    a_gp = I("a_gp", [SG, 3 * 64]); a_sp = I("a_sp", [128, 3 * NPT])
    b_ssm = I("b_ssm", [SG, 2 * 64 * 16]); c_ssm = I("c_ssm", [2, SG * 16, 64]); d_ssm = I("d_ssm", [128, NCB])
    ident_d = I("ident", [128, 128]); pm_d = I("pm", [128, 128]); mask_d = I("mask2", [128, 256])
    bmask_d = I("bmask", [128, 8]); cmask_d = I("cmask", [128, 4 * 128]); trow_d = I("trow", [128, 2 * T])
    pos_d = I("posrow", [128, T + 1]); invf_d = I("invf", [128, 1])

    O = kb.outp
    yT = O("yT", [DM, T]); ys_o = O("ys", [128, NCH])
    oa = [O("oa%d" % g, [LG[g], 2 * AH * 128]) for g in range(3)]
    sa = [O("sa%d" % g, [LG[g], 2 * AH * 128]) for g in range(3)]
    ossm_p = O("ossm_p", [128, NPT * 2]); ossm_s = O("ossm_s", [128, NPT * 2])
    oconv_p = O("oconv_p", [128, NCH * 30]); oconv_s = O("oconv_s", [128, NCH * 30])
    offn_p = O("offn_p", [128, 2 * 2 * NFF * 2]); offn_s = O("offn_s", [128, 2 * 2 * NFF * 2])

    S = kb.scratch
    XA = S("XA", [DM, T]); XB = S("XB", [DM, T])
    Z = S("Z", [NMIN * 128, T])
    CAT = S("CAT", [DM, T], BF16); YG = S("YG", [c["DB"], T], BF16)
    ACTS = S("ACTS", [NFF * 128, T], BF16)
    US = S("US", [DM, 32 + T]); YC = S("YC", [DM, T]); SW = S("SW", [DM, T], BF16)
    BBT = S("BBT", [2, SG * 16, 64])

    st = ExitStack()
    with st:
        P = Prog(nc, st)
        kb.P = P
        Ar = Arena(nc, st, ARENA_BYTES)
        psum = [st.enter_context(nc.psum_tensor("ps%d" % i, [128, 512], F32)) for i in range(8)]
        pb = [P.buf("!ps%d" % i) for i in range(8)]

        def pers(shape, dt=F32, name="p"):
            return Ar.alloc(shape, dt), P.buf("!" + name)
        ident, b_ident = pers([128]); pm_sb, b_pm = pers([128]); ones_f, b_ones = pers([128])
        ones_bf, b_onesb = pers([128], BF16)
        mask_f, b_mask = pers([256])
        gmix_sb, b_gmix = pers([2 * NCH]); gffn_sb, b_gffn = pers([2 * NCH]); gfin_sb, b_gfin = pers([NCH])
        bglu_sb, b_bglu = pers([NCB]); bpw1_sb, b_bpw1 = pers([2 * NCH]); bdw_sb, b_bdw = pers([NCH])
        lng_sb, b_lng = pers([NCH]); lnb_sb, b_lnb = pers([NCH]); bpw2_sb, b_bpw2 = pers([NCH])
        wdw_sb, b_wdw = pers([NCH, 31]); fdw_sb, b_fdw = pers([2, 2 * NFF, 3]); fb_sb, b_fb = pers([2, 2 * NFF])
        dssm_sb, b_dssm = pers([NCB]); invf_sb, b_invf = pers([1])
        xs_sb, b_xs = pers([NCH])
        hns, b_hns = pers([max(NCH, NFF)], BF16)
        zs, b_zs = pers([NMIN]); zsr, b_zsr = pers([NMIN])
        cs_s, b_css = pers([2])
        zprev, b_zprev = pers([2 * NFF, 2])
        eps_rms, b_eps = pers([4])
        halfpi = eps_rms[:, 2:3]
        cats, b_cats = pers([NCH]); ygs, b_ygs = pers([NCB])
        Ar.freeze()

        def ldc(dst, src, buf):
            kb.ld(dst, src, buf)
        ldc(ident, ident_d, b_ident); ldc(pm_sb, pm_d, b_pm); ldc(mask_f, mask_d, b_mask)
        ldc(gmix_sb, g_mix, b_gmix); ldc(gffn_sb, g_ffn, b_gffn); ldc(gfin_sb, g_fin, b_gfin)
        ldc(bglu_sb, b_glu, b_bglu); ldc(bpw1_sb, b_pw1, b_bpw1); ldc(bdw_sb, b_dw, b_bdw)
        ldc(lng_sb, ln_g, b_lng); ldc(lnb_sb, ln_b, b_lnb); ldc(bpw2_sb, b_pw2, b_bpw2)
        ldc(wdw_sb, w_dw.rearrange("p (c j) -> p c j", j=31), b_wdw)
        ldc(fdw_sb, ffn_dw.rearrange("p (l m j) -> p l m j", l=2, j=3), b_fdw)
        ldc(fb_sb, ffn_b.rearrange("p (l m) -> p l m", l=2), b_fb)
        ldc(dssm_sb, d_ssm, b_dssm); ldc(invf_sb, invf_d, b_invf); ldc(xs_sb, xs_d, b_xs)
        kb.memset("dve", ones_f, 1.0 / DM, [b_ones])
        kb.memset("dve", ones_bf, 1.0, [b_onesb])
        kb.memset("dve", eps_rms[:, 0:1], 1e-6, [b_eps])
        kb.memset("dve", eps_rms[:, 1:2], 1e-5, [b_eps])
        kb.memset("dve", eps_rms[:, 2:3], math.pi / 2.0, [b_eps])
        P.barrier()

        ctx = dict(kb=kb, P=P, Ar=Ar, psum=psum, pb=pb)

        def rstd_from_ss(ps_ap, pbuf, n, eps_col, out_ap, out_buf, tmp_ap, tmp_buf):
            kb.act(tmp_ap, ps_ap, AF.Sqrt, [pbuf, b_eps], [tmp_buf], bias=eps_col)
            kb.A("dve", lambda e: e.reciprocal(out=out_ap, in_=tmp_ap), [tmp_buf], [out_buf])

        def norm_tokens(src, gsb, gbuf, goff, tok0, ntok, emit_chunk):
            X = Ar.alloc([NCH, 256]); bX = [P.buf("X%d" % i) for i in range(4)]
            sq = [Ar.alloc([256]) for _ in range(2)]; bsq = [P.buf("sq") for _ in range(2)]
            rs = Ar.alloc([256]); brs = P.buf("rs"); tmp = Ar.alloc([256]); btmp = P.buf("tmp")
            srcv = src.rearrange("(c p) t -> p c t", p=128)
            q4 = max(1, NCH // 4)
            for tgi in range(ntok // 256):
                t0 = tok0 + tgi * 256
                for qd in range(0, NCH, q4):
                    kb.ld(X[:, qd:qd + q4, :], srcv[:, qd:qd + q4, t0:t0 + 256], bX[(qd // q4) % 4])
                for cc_ in range(NCH):
                    s = cc_ % 2
                    bx = bX[(cc_ // q4) % 4]
                    kb.act(sq[s], X[:, cc_, :], AF.Square, [bx], [bsq[s]])
                    kb.mm(psum[7][:, 0:256], ones_f, sq[s], cc_ == 0, cc_ == NCH - 1, [b_ones, bsq[s]], [pb[7]])
                rstd_from_ss(psum[7][:, 0:256], pb[7], 256, eps_rms[:, 0:1], rs, brs, tmp, btmp)
                for cc_ in range(NCH):
                    emit_chunk(cc_, tgi, X[:, cc_, :], bX[(cc_ // q4) % 4], rs, brs)

        def norm_to_in(src, gsb, gbuf, goff, sbi, in_tile, in_buf):
            def emit(cc_, tgi, xin, bx, rs, brs):
                eng = "dve" if cc_ % 2 == 0 else "pool"
                kb.stt(eng, in_tile[:, cc_, tgi * 256:(tgi + 1) * 256], xin, gsb[:, goff + cc_:goff + cc_ + 1], rs,
                       ALU.mult, ALU.mult, [bx, gbuf, brs], [in_buf])
            norm_tokens(src, gsb, gbuf, goff, sbi * 1024, 1024, emit)

        def norm_sample(gsb, gbuf, goff):
            t1 = Ar.alloc([NCH]); bt1 = P.buf("ns1"); t2 = Ar.alloc([2]); bt2 = P.buf("ns2")
            kb.tt("dve", t1, xs_sb, xs_sb, ALU.mult, [b_xs], [bt1])
            kb.A("dve", lambda e: e.reduce_sum(out=t2[:, 0:1], in_=t1, axis=AX.X), [bt1], [bt2])
            kb.mm(psum[7][:, 256:257], ones_f, t2[:, 0:1], True, True, [b_ones, bt2], [pb[7]])
            kb.act(t2[:, 1:2], psum[7][:, 256:257], AF.Sqrt, [pb[7], b_eps], [bt2], bias=eps_rms[:, 0:1])
            kb.A("dve", lambda e: e.reciprocal(out=t2[:, 0:1], in_=t2[:, 1:2]), [bt2], [bt2])
            kb.stt("dve", hns[:, 0:NCH], xs_sb, t2[:, 0:1], gsb[:, goff:goff + NCH], ALU.mult, ALU.mult,
                   [b_xs, bt2, gbuf], [b_hns])

        def load_in(src_bf, nk, tok0, ntok, in_tile, in_buf):
            srcv = src_bf.rearrange("(c p) t -> p c t", p=128)
            step = max(1, nk // 4)
            for k0 in range(0, nk, step):
                k1 = min(nk, k0 + step)
                kb.ld(in_tile[:, k0:k1, 0:ntok], srcv[:, k0:k1, tok0:tok0 + ntok], in_buf)

        lin_state = {"slot": 0, "bank": 0}

        def linear(Wt_of, steps, nk, in_tile, in_buf, ntok, evac, wslots, wbufs, sample_cols=None):
            ntg = ntok // 512
            for si, ms in enumerate(steps):
                slot = lin_state["slot"] % len(wslots)
                lin_state["slot"] += 1
                wv = wslots[slot]
                wb = wbufs[slot]
                for j, m in enumerate(ms):
                    kb.ld(wv[:, j * nk * 128:(j + 1) * nk * 128], Wt_of(m), wb, q="pool")
                for tgi in range(ntg):
                    banks = []
                    for j in range(len(ms)):
                        banks.append(lin_state["bank"] % 4)
                        lin_state["bank"] += 1
                    for j, m in enumerate(ms):
                        bi = banks[j]
                        for k in range(nk):
                            kb.mm(psum[bi][:, :], wv[:, (j * nk + k) * 128:(j * nk + k + 1) * 128],
                                  in_tile[:, k, tgi * 512:(tgi + 1) * 512], k == 0, k == nk - 1,
                                  [wb, in_buf], [pb[bi]])
                    evac(si, tgi, [psum[b][:, :] for b in banks], [pb[b] for b in banks])
                if sample_cols is not None:
                    for j, m in enumerate(ms):
                        col = sample_cols[si][j]
                        for k in range(nk):
                            kb.mm(psum[6][:, col:col + 1], wv[:, (j * nk + k) * 128:(j * nk + k + 1) * 128],
                                  hns[:, k:k + 1], k == 0, k == nk - 1, [wb, b_hns], [pb[6]])

        def wslots_alloc(n, elems):
            sl = [Ar.alloc([elems], BF16) for _ in range(n)]
            return sl, [P.buf("w%d" % i) for i in range(n)]

        ctx.update(locals())
        from types import SimpleNamespace
        E = SimpleNamespace(**ctx)
        phases = [lambda: phase_layer0_proj(E), lambda: phase_attention(E), lambda: phase_ssm(E),
                  lambda: phase_glu_out(E), lambda: phase_ffn(E, 0, XB, XA), lambda: phase_conv(E),
                  lambda: phase_ffn(E, 1, XB, XA), lambda: phase_final(E)]
        for ph in phases[:cfg.get("STOP", len(phases))]:
            ph()
        P.barrier()
        P.emit()
    return kb


C1_2PI = 6.28125
C2_2PI = TWO_PI - 6.28125
PI_SAFE = 3.1415925


def sincos(E, ang, n, S_out, C_out, bufs_in, buf_out, ki, kf, ab, bscr, engs=("dve", "dve")):
    kb = E.kb
    e0, e1 = engs
    kb.ts(e0, ki, ang, 1.0 / TWO_PI, None, ALU.mult, None, bufs_in, [bscr])
    kb.cp(e0, kf, ki, [bscr], [bscr])
    kb.stt(e1, ab, kf, -C1_2PI, ang, ALU.mult, ALU.add, [bscr] + bufs_in, [bscr])
    kb.stt(e1, ab, kf, -C2_2PI, ab, ALU.mult, ALU.add, [bscr], [bscr])
    kb.ts(e1, ab, ab, -PI_SAFE, PI_SAFE, ALU.max, ALU.min, [bscr], [bscr])
    kb.act(S_out, ab, AF.Sin, [bscr], [buf_out])
    kb.act(C_out, ab, AF.Sin, [bscr], [buf_out], scale=0.5)
    kb.tt(e0, C_out, C_out, C_out, ALU.mult, [buf_out], [buf_out])
    kb.ts(e0, C_out, C_out, -2.0, 1.0, ALU.mult, ALU.add, [buf_out], [buf_out])


def phase_layer0_proj(E):
    kb, P, Ar, psum, pb, c = E.kb, E.P, E.Ar, E.psum, E.pb, E.kb.c
    T, NCH, AH, NMIN, NQKV = c["T"], c["NCH"], c["AH"], c["NMIN"], c["NQKV"]
    Ar.reset()
    Ct = Ar.alloc([T + 1]); St = Ar.alloc([T + 1]); b_rope = P.buf("rope")
    keep = Ar.off
    ang = Ar.alloc([T + 1]); ki = Ar.alloc([T + 1], I32); kf = Ar.alloc([T + 1]); ab = Ar.alloc([T + 1])
    b_ang = P.buf("ang"); b_scr = P.buf("scr")
    kb.ld(ang, E.pos_d, b_ang)
    kb.ts("dve", ang, ang, E.invf_sb[:, 0:1], None, ALU.mult, None, [b_ang, E.b_invf], [b_ang])
    sincos(E, ang, T + 1, St, Ct, [b_ang], b_rope, ki, kf, ab, b_scr)
    P.barrier()
    Ar.off = keep
    E.Ct, E.St = Ct, St

    IN = Ar.alloc([NCH, 1024], BF16); b_in = P.buf("in")
    wsl, wbf = E.wslots_alloc(3, NCH * 128)
    stg = [Ar.alloc([1024]) for _ in range(3)]; bstg = [P.buf("stg%d" % i) for i in range(3)]
    xr = [Ar.alloc([512]) for _ in range(2)]; bxr = [P.buf("xr%d" % i) for i in range(2)]
    t2 = [Ar.alloc([512]) for _ in range(2)]; bt2 = [P.buf("t2%d" % i) for i in range(2)]
    keep2 = Ar.off
    cnt = {"r": 0}

    def kind(m):
        if m >= NQKV:
            return "u"
        return "qkv"[(m // AH) % 3]

    for sbi in range(T // 1024):
        if sbi > 0:
            P.barrier()
        Ar.off = keep2
        E.norm_to_in(E.xT, E.gmix_sb, E.b_gmix, 0, sbi, IN, b_in)
        if sbi == 0:
            E.norm_sample(E.gmix_sb, E.b_gmix, 0)

        def evac(si, tgi, pss, pbs, sbi=sbi):
            m = si
            sl = m % 3
            ps, pbuf = pss[0], pbs[0]
            tok0 = sbi * 1024 + tgi * 512
            dst = stg[sl][:, tgi * 512:(tgi + 1) * 512]
            if kind(m) in "qk":
                s = cnt["r"] % 2
                cnt["r"] += 1
                kb.cp("act", xr[s], ps, [pbuf], [bxr[s]])
                kb.mm(psum[4 + s][:, :], E.pm_sb, xr[s], True, True, [E.b_pm, bxr[s]], [pb[4 + s]])
                kb.tt("dve", t2[s], psum[4 + s][:, :], St[:, tok0:tok0 + 512], ALU.mult, [pb[4 + s], b_rope], [bt2[s]])
                kb.tt("pool", dst, xr[s], Ct[:, tok0:tok0 + 512], ALU.mult, [bxr[s], b_rope], [bstg[sl]])
                kb.tt("dve", dst, dst, t2[s], ALU.add, [bstg[sl], bt2[s]], [bstg[sl]])
            else:
                kb.cp("act" if (m + tgi) % 2 == 0 else "dve", dst, ps, [pbuf], [bstg[sl]])
            if tgi == 1:
                kb.stq(E.Z[m * 128:(m + 1) * 128, sbi * 1024:(sbi + 1) * 1024], stg[sl], bstg[sl])

        E.linear(lambda m: E.w_in_t[m], [[m] for m in range(NMIN)], NCH, IN, b_in, 1024, evac, wsl, wbf,
                 sample_cols=[[m] for m in range(NMIN)] if sbi == 0 else None)
        if sbi == 0:
            kb.cp("act", E.zs, psum[6][:, 0:NMIN], [pb[6]], [E.b_zs])
            kb.mm(psum[5][:, 0:NMIN], E.pm_sb, E.zs, True, True, [E.b_pm, E.b_zs], [pb[5]])
            kb.ts("dve", E.zsr, E.zs, Ct[:, T:T + 1], None, ALU.mult, None, [E.b_zs, b_rope], [E.b_zsr])
            kb.stt("dve", E.zsr, psum[5][:, 0:NMIN], St[:, T:T + 1], E.zsr, ALU.mult, ALU.add,
                   [pb[5], b_rope, E.b_zsr], [E.b_zsr])
    P.barrier()


def phase_attention(E):
    kb, P, Ar, psum, pb, c = E.kb, E.P, E.Ar, E.psum, E.pb, E.kb.c
    T, NCH, AH, NMIN = c["T"], c["NCH"], c["AH"], c["NMIN"]
    LG = (128, 512, 2048)
    Ar.reset()
    scale = 1.0 / math.sqrt(128.0)
    mask_bf = Ar.alloc([2, 128], BF16); b_mbf = P.buf("mbf")
    kb.cp("dve", mask_bf, E.mask_f.rearrange("p (a b) -> p a b", a=2), [E.b_mask], [b_mbf])
    qkv32 = [[Ar.alloc([T]) for _ in range(3)] for _ in range(2)]
    bq32 = [[P.buf("q32") for _ in range(3)] for _ in range(2)]
    qbf = [Ar.alloc([16, 128], BF16) for _ in range(2)]; bqbf = [P.buf("qbf") for _ in range(2)]
    kbf = [Ar.alloc([16, 128], BF16) for _ in range(2)]; bkbf = [P.buf("kbf") for _ in range(2)]
    vtok = [Ar.alloc([16, 128], BF16) for _ in range(2)]; bvtok = [P.buf("vtok") for _ in range(2)]
    acc_o = Ar.alloc([T]); acc_l = Ar.alloc([T]); b_acc = P.buf("acc")
    attn_bf = Ar.alloc([T], BF16); b_attn = P.buf("attnbf")
    e32 = [Ar.alloc([128]) for _ in range(4)]; be32 = [P.buf("e32") for _ in range(4)]
    em = [Ar.alloc([128], BF16) for _ in range(4)]; bem = [P.buf("em") for _ in range(4)]
    kvo = [Ar.alloc([4, 128]) for _ in range(4)]; bkvo = [P.buf("kvo") for _ in range(4)]
    cch = Ar.alloc([256]); b_cch = P.buf("cch")
    ktT = Ar.alloc([128]); b_ktT = P.buf("ktT")
    es = Ar.alloc([4]); b_es = P.buf("es")
    accs_o = Ar.alloc([AH]); accs_l = Ar.alloc([AH]); b_accs = P.buf("accs")
    kvs = Ar.alloc([2 * AH]); b_kvs = P.buf("kvs"); kvsT = Ar.alloc([128]); b_kvsT = P.buf("kvsT")
    cats = E.cats
    it = 0
    ev = 0
    DBG = c.get("DBG", 0)
    for h in range(AH):
        for g in range(3):
            D = DIL[g]
            nblk = T // (128 * D)
            s = it % 2
            it += 1
            mq, mk, mv = ((g * 3 + j) * AH + h for j in range(3))
            for j, m in enumerate((mq, mk, mv)):
                kb.ld(qkv32[s][j], E.Z[m * 128:(m + 1) * 128, :], bq32[s][j])

            def blkview(ap):
                return ap.rearrange("p (i a r) -> p r i a", i=nblk, a=128, r=D)
            kb.cp("pool", qbf[s].rearrange("p (r i) a -> p r i a", r=D), blkview(qkv32[s][0]), [bq32[s][0]], [bqbf[s]])
            kb.cp("pool", kbf[s].rearrange("p (r i) a -> p r i a", r=D), blkview(qkv32[s][1]), [bq32[s][1]], [bkbf[s]])
            vv = blkview(qkv32[s][2]); kv_ = blkview(qkv32[s][1])
            out_first_i = {0: nblk - 1, 1: nblk - 1, 2: 0}[g]
            for b0 in range(0, 16, 4):
                if DBG & 8:
                    break
                bank = 4 + (ev % 2)
                for bb in range(4):
                    r, i = divmod(b0 + bb, nblk)
                    kb.tr(psum[bank][:, bb * 128:(bb + 1) * 128], vv[:, r, i, :], E.ident, [bq32[s][2], E.b_ident], [pb[bank]])
                kb.cp("act", vtok[s][:, b0:b0 + 4, :], psum[bank][:, :].rearrange("p (b d) -> p b d", b=4), [pb[bank]], [bvtok[s]])
                outblks = [(bb,) + divmod(b0 + bb, nblk) for bb in range(4) if divmod(b0 + bb, nblk)[1] >= out_first_i]
                if DBG & 2:
                    outblks = []
                if outblks:
                    ko = ev % 4
                    if not (DBG & 64):
                        kb.cp("act", kvo[ko], psum[bank][:, :].rearrange("p (b d) -> p b d", b=4), [pb[bank]], [bkvo[ko]])
                    for (bb, r, i) in outblks:
                        a0 = 128 * (i - out_first_i)
                        dst = E.oa[g].rearrange("(a r) (k h d) -> r a k h d", r=D, k=2, h=AH)[r, a0:a0 + 128, 1, h, :]
                        if DBG & 32:
                            continue
                        tk = kb.stq(dst, kvo[ko][:, bb, :], bkvo[ko])
                        if DBG & 16:
                            P.fence("sp", tk)
                ev += 1
                if outblks:
                    bank = 4 + (ev % 2)
                    for bb in range(4):
                        r, i = divmod(b0 + bb, nblk)
                        kb.tr(psum[bank][:, bb * 128:(bb + 1) * 128], kv_[:, r, i, :], E.ident, [bq32[s][1], E.b_ident], [pb[bank]])
                    ko = ev % 4
                    if not (DBG & 64):
                        kb.cp("act", kvo[ko], psum[bank][:, :].rearrange("p (b d) -> p b d", b=4), [pb[bank]], [bkvo[ko]])
                    for (bb, r, i) in outblks:
                        a0 = 128 * (i - out_first_i)
                        dst = E.oa[g].rearrange("(a r) (k h d) -> r a k h d", r=D, k=2, h=AH)[r, a0:a0 + 128, 0, h, :]
                        if DBG & 32:
                            continue
                        tk = kb.stq(dst, kvo[ko][:, bb, :], bkvo[ko])
                        if DBG & 16:
                            P.fence("sp", tk)
                    ev += 1
            accv_o = blkview(acc_o); accv_l = blkview(acc_l)
            pr = 0
            for b0 in range(0, 16, 4):
                if DBG & 4:
                    break
                bo = (b0 // 4) % 2
                bl = 2 + bo
                for bb in range(4):
                    qb = b0 + bb
                    r, i = divmod(qb, nblk)
                    kbs = ([(qb - 1, 1)] if i > 0 else []) + [(qb, 0)]
                    for n_, (kbk, which) in enumerate(kbs):
                        sl = pr % 4
                        pr += 1
                        sbank = 4 + (sl % 2)
                        scol = (sl // 2) * 128
                        kb.mm(psum[sbank][:, scol:scol + 128], kbf[s][:, kbk, :], qbf[s][:, qb, :], True, True,
                              [bkbf[s], bqbf[s]], [pb[sbank]])
                        kb.act(e32[sl], psum[sbank][:, scol:scol + 128], AF.Exp, [pb[sbank]], [be32[sl]], scale=scale)
                        kb.tt("dve" if sl % 2 == 0 else "pool", em[sl], e32[sl], E.mask_f[:, which * 128:(which + 1) * 128],
                              ALU.mult, [be32[sl], E.b_mask], [bem[sl]])
                        first = n_ == 0
                        last = n_ == len(kbs) - 1
                        kb.mm(psum[bo][:, bb * 128:(bb + 1) * 128], vtok[s][:, kbk, :], em[sl], first, last,
                              [bvtok[s], bem[sl]], [pb[bo]])
                        kb.mm(psum[bl][:, bb * 128:(bb + 1) * 128], E.ones_bf, em[sl], first, last,
                              [E.b_onesb, bem[sl]], [pb[bl]])
                if nblk >= 4:
                    r, i0 = divmod(b0, nblk)
                    do = accv_o[:, r, i0:i0 + 4, :]; dl = accv_l[:, r, i0:i0 + 4, :]
                else:
                    do = accv_o[:, b0:b0 + 4, 0, :]; dl = accv_l[:, b0:b0 + 4, 0, :]
                so = psum[bo][:, :].rearrange("p (b a) -> p b a", b=4); sl_ = psum[bl][:, :].rearrange("p (b a) -> p b a", b=4)
                if g == 0:
                    kb.cp("act", do, so, [pb[bo]], [b_acc])
                    kb.cp("dve", dl, sl_, [pb[bl]], [b_acc])
                else:
                    kb.tt("dve", do, do, so, ALU.add, [pb[bo], b_acc], [b_acc])
                    kb.tt("dve", dl, dl, sl_, ALU.add, [pb[bl], b_acc], [b_acc])
            if DBG & 1:
                continue
            cv = E.cache[g].rearrange("(a r) (k h d) -> r a k h d", r=D, k=2, h=AH)
            kb.ld(cch[:, 0:128], cv[0, :, 0, h, :], b_cch)
            kb.ld(cch[:, 128:256], cv[0, :, 1, h, :], b_cch)
            if h == 0:
                L = LG[g]
                P.dma("sp", (lambda o, i_: (lambda e: e.dma_start(out=o, in_=i_)))(E.sa[g][0:L - 1, :], E.cache[g][1:L, :]),
                      reads=(), writes=(), sembuf=b_kvs, store=True)
            kc_ = cch[:, 0:128]; vc_ = cch[:, 128:256]
            kb.tr(psum[6][:, 0:128], kc_, E.ident, [b_cch, E.b_ident], [pb[6]])
            kb.cp("act", ktT, psum[6][:, 0:128], [pb[6]], [b_ktT])
            qs = E.zsr[:, mq:mq + 1]; ks = E.zsr[:, mk:mk + 1]; vs = E.zs[:, mv:mv + 1]
            kb.mm(psum[7][:, 0:1], ktT, qs, True, True, [b_ktT, E.b_zsr], [pb[7]])
            kb.mm(psum[7][:, 1:2], bcast_free(ks, 128), qs, True, True, [E.b_zsr], [pb[7]])
            kb.act(es[:, 0:2], psum[7][:, 0:2], AF.Exp, [pb[7]], [b_es], scale=scale)
            kb.mm(psum[7][:, 2:3], vc_, es[:, 0:1], True, True, [b_cch, b_es], [pb[7]])
            kb.mm(psum[7][:, 3:4], E.ones_f, es[:, 0:1], True, True, [E.b_ones, b_es], [pb[7]])
            kb.stt("dve", es[:, 2:3], vs, es[:, 1:2], psum[7][:, 2:3], ALU.mult, ALU.add, [E.b_zs, b_es, pb[7]], [b_es])
            kb.stt("dve", es[:, 3:4], psum[7][:, 3:4], float(c["DM"]), es[:, 1:2], ALU.mult, ALU.add, [pb[7], b_es], [b_es])
            if g == 0:
                kb.cp("dve", accs_o[:, h:h + 1], es[:, 2:3], [b_es], [b_accs])
                kb.cp("dve", accs_l[:, h:h + 1], es[:, 3:4], [b_es], [b_accs])
            else:
                kb.tt("dve", accs_o[:, h:h + 1], accs_o[:, h:h + 1], es[:, 2:3], ALU.add, [b_es, b_accs], [b_accs])
                kb.tt("dve", accs_l[:, h:h + 1], accs_l[:, h:h + 1], es[:, 3:4], ALU.add, [b_es, b_accs], [b_accs])
        kb.A("dve", lambda e: e.reciprocal(out=acc_l, in_=acc_l), [b_acc], [b_acc])
        kb.tt("dve", attn_bf, acc_o, acc_l, ALU.mult, [b_acc], [b_attn])
        kb.stq(E.CAT[h * 128:(h + 1) * 128, :], attn_bf, b_attn)
    if DBG & 1:
        P.barrier()
        return
    kb.A("dve", lambda e: e.reciprocal(out=accs_l, in_=accs_l), [b_accs], [b_accs])
    kb.tt("dve", cats[:, 0:AH], accs_o, accs_l, ALU.mult, [b_accs], [E.b_cats])
    for g in range(3):
        L = LG[g]
        for h in range(AH):
            mk = (g * 3 + 1) * AH + h; mv = (g * 3 + 2) * AH + h
            kb.cp("dve", kvs[:, h:h + 1], E.zsr[:, mk:mk + 1], [E.b_zsr], [b_kvs])
            kb.cp("dve", kvs[:, AH + h:AH + h + 1], E.zs[:, mv:mv + 1], [E.b_zs], [b_kvs])
        kb.tr(psum[6][0:2 * AH, 0:128], kvs, E.ident, [b_kvs, E.b_ident], [pb[6]])
        kb.cp("act", kvsT[0:2 * AH, :], psum[6][0:2 * AH, 0:128], [pb[6]], [b_kvsT])
        kb.stq(E.sa[g][L - 1:L, :].rearrange("o (j d) -> (o j) d", d=128), kvsT[0:2 * AH, :], b_kvsT)
    P.barrier()


def bc_mid(ap2, n):
    return bass.AP(ap2.tensor, ap2.offset, [list(ap2.ap[0]), [0, n], list(ap2.ap[1])])


def bc_last(ap2, n):
    return bass.AP(ap2.tensor, ap2.offset, [list(ap2.ap[0]), list(ap2.ap[1]), [0, n]])


def ssm_params(E, src, Pn, F, tag):
    kb, P, Ar = E.kb, E.P, E.Ar
    b = P.buf("sp" + tag)
    names = ["dt", "mag", "ang", "th", "sn", "cs", "abre", "abim", "den", "m1", "t1", "t2", "zre", "zim", "kf"]
    t = {n: Ar.alloc([F])[0:Pn] for n in names}
    ki = Ar.alloc([F], I32)[0:Pn]
    are, aim, ldt = src[:, 0, :], src[:, 1, :], src[:, 2, :]
    sb_ = E.b_sp_src
    kb.act(t["dt"], ldt, AF.Exp, [sb_], [b])
    kb.tt("dve", t["mag"], are, t["dt"], ALU.mult, [sb_, b], [b])
    kb.act(t["mag"], t["mag"], AF.Exp, [b], [b])
    kb.tt("dve", t["ang"], aim, t["dt"], ALU.mult, [sb_, b], [b])
    sincos(E, t["ang"], F, t["sn"], t["cs"], [b], b, ki, t["kf"], t["th"], b)
    kb.tt("dve", t["abre"], t["mag"], t["cs"], ALU.mult, [b], [b])
    kb.tt("dve", t["abim"], t["mag"], t["sn"], ALU.mult, [b], [b])
    kb.tt("dve", t["den"], are, are, ALU.mult, [sb_], [b])
    kb.tt("dve", t["t1"], aim, aim, ALU.mult, [sb_], [b])
    kb.tt("dve", t["den"], t["den"], t["t1"], ALU.add, [b], [b])
    kb.A("dve", lambda e: e.reciprocal(out=t["den"], in_=t["den"]), [b], [b])
    kb.ts("dve", t["m1"], t["abre"], -1.0, None, ALU.add, None, [b], [b])
    kb.tt("dve", t["t1"], t["m1"], are, ALU.mult, [b, sb_], [b])
    kb.tt("dve", t["t2"], t["abim"], aim, ALU.mult, [b, sb_], [b])
    kb.tt("dve", t["t1"], t["t1"], t["t2"], ALU.add, [b], [b])
    kb.tt("dve", t["zre"], t["t1"], t["den"], ALU.mult, [b], [b])
    kb.tt("dve", t["t1"], t["abim"], are, ALU.mult, [b, sb_], [b])
    kb.tt("dve", t["t2"], t["m1"], aim, ALU.mult, [b, sb_], [b])
    kb.tt("dve", t["t1"], t["t1"], t["t2"], ALU.subtract, [b], [b])
    kb.tt("dve", t["zim"], t["t1"], t["den"], ALU.mult, [b], [b])
    t["buf"] = b
    t["ki"] = ki
    return t


def phase_ssm(E):
    kb, P, Ar, psum, pb, c = E.kb, E.P, E.Ar, E.psum, E.pb, E.kb.c
    T, NCB, NPT, SG, NQKV = c["T"], c["NCB"], c["NPT"], c["SG"], c["NQKV"]
    Ar.reset()
    src_gp = Ar.alloc([3, 64])[0:SG]; E.b_sp_src = P.buf("srcgp")
    kb.ld(src_gp, E.a_gp.rearrange("g (k f) -> g k f", k=3), E.b_sp_src)
    pg = ssm_params(E, src_gp, SG, 64, "gp")
    braw = Ar.alloc([2, 64, 16])[0:SG]; b_braw = P.buf("braw")
    kb.ld(braw, E.b_ssm.rearrange("g (r p c) -> g r p c", r=2, p=64), b_braw)
    BT = Ar.alloc([2, 16, 64])[0:SG]; b_BT = P.buf("BT")
    tA = Ar.alloc([16, 64])[0:SG]; tB = Ar.alloc([16, 64])[0:SG]; b_tAB = P.buf("tAB")
    br = braw[:, 0].rearrange("g p c -> g c p"); bi = braw[:, 1].rearrange("g p c -> g c p")
    zre_b = bc_mid(pg["zre"], 16); zim_b = bc_mid(pg["zim"], 16)
    bz = pg["buf"]
    kb.tt("dve", tA, br, zre_b, ALU.mult, [b_braw, bz], [b_tAB])
    kb.tt("dve", tB, bi, zim_b, ALU.mult, [b_braw, bz], [b_tAB])
    kb.tt("dve", BT[:, 0], tA, tB, ALU.subtract, [b_tAB], [b_BT])
    kb.tt("dve", tA, bi, zre_b, ALU.mult, [b_braw, bz, b_BT], [b_tAB])
    kb.tt("dve", tB, br, zim_b, ALU.mult, [b_braw, bz], [b_tAB])
    kb.tt("dve", BT[:, 1], tA, tB, ALU.add, [b_tAB], [b_BT])
    for ri in range(2):
        kb.stq(E.BBT[ri].rearrange("(g c) p -> g c p", c=16), BT[:, ri], b_BT)
    P.barrier()
    Ar.reset()
    src_sp = Ar.alloc([3, NPT]); E.b_sp_src = P.buf("srcsp")
    kb.ld(src_sp, E.a_sp.rearrange("p (k f) -> p k f", k=3), E.b_sp_src)
    ps_ = ssm_params(E, src_sp, 128, NPT, "sp")
    bsp = ps_["buf"]
    th = ps_["th"]; mag = ps_["mag"]
    a1 = Ar.alloc([NPT]); a1k = Ar.alloc([NPT], I32); a1f = Ar.alloc([NPT])
    kb.ts("dve", a1f, th, 64.0, None, ALU.mult, None, [bsp], [bsp])
    kb.ts("dve", a1k, a1f, 1.0 / TWO_PI, None, ALU.mult, None, [bsp], [bsp])
    kb.cp("dve", a1, a1k, [bsp], [bsp])
    kb.stt("dve", a1f, a1, -C1_2PI, a1f, ALU.mult, ALU.add, [bsp], [bsp])
    kb.stt("dve", a1, a1, -C2_2PI, a1f, ALU.mult, ALU.add, [bsp], [bsp])
    trow = Ar.alloc([2, T]); b_trow = P.buf("trow")
    kb.ld(trow, E.trow_d.rearrange("p (k t) -> p k t", k=2), b_trow)
    bmask = Ar.alloc([8]); cmask = Ar.alloc([4, 128]); b_msk = P.buf("msk")
    kb.ld(bmask, E.bmask_d, b_msk); kb.ld(cmask, E.cmask_d.rearrange("p (j q) -> p j q", j=4), b_msk)
    h0 = Ar.alloc([NPT, 2]); b_h0 = P.buf("h0"); kb.ld(h0, E.st_ssm.rearrange("p (n k) -> p n k", k=2), b_h0)
    hout_p = Ar.alloc([NPT, 2]); b_houtp = P.buf("houtp"); hout_s = Ar.alloc([NPT, 2]); b_houts = P.buf("houts")
    hs_bf = Ar.alloc([NPT, 2], BF16); b_hsbf = P.buf("hsbf")
    us_bf = Ar.alloc([NCB], BF16); b_usbf = P.buf("usbf")
    kb.cp("dve", us_bf, E.zs[:, NQKV:NQKV + NCB], [E.b_zs], [b_usbf])
    cosT = Ar.alloc([T]); sinT = Ar.alloc([T]); b_trig = P.buf("trig")
    angT = Ar.alloc([T]); kiT = Ar.alloc([T], I32); kfT = Ar.alloc([T]); b_tscr = P.buf("tscr")
    u32 = [Ar.alloc([T]) for _ in range(2)]; bu32 = [P.buf("u32") for _ in range(2)]
    ubf = [Ar.alloc([T], BF16) for _ in range(2)]; bubf = [P.buf("ubf") for _ in range(2)]
    bt = [Ar.alloc([T]) for _ in range(2)]; b_bt = [P.buf("btre"), P.buf("btim")]
    gg = [Ar.alloc([T]) for _ in range(2)]; b_gg = [P.buf("ggre"), P.buf("ggim")]
    tmpa = Ar.alloc([512]); tmpb = Ar.alloc([512]); b_tmpa = P.buf("tmpa"); b_tmpb = P.buf("tmpb")
    hbf = [[Ar.alloc([T], BF16) for _ in range(2)] for _ in range(4)]; bhbf = [P.buf("hbf%d" % j) for j in range(4)]
    rows = [Ar.alloc([64]) for _ in range(2)]; b_rows = P.buf("rows")
    BD = [Ar.alloc([8, 64], BF16) for _ in range(2)]; b_BD = P.buf("BD")
    crow = [Ar.alloc([64]) for _ in range(2)]; b_crow = P.buf("crow")
    cdup = Ar.alloc([2, 64]); b_cdup = P.buf("cdup")
    CL = [Ar.alloc([4, 128], BF16) for _ in range(2)]; b_CL = P.buf("CL")
    yv = Ar.alloc([T]); ysq = Ar.alloc([T]); b_yv = P.buf("yv"); b_ysq = P.buf("ysq")
    ygbf = Ar.alloc([T], BF16); b_ygbf = P.buf("ygbf")
    ys_t = Ar.alloc([NCB]); ys_q = Ar.alloc([NCB]); b_yst = P.buf("yst")
    GC = 2.0 * math.sqrt(2.0 / math.pi)

    for cc in range(NCB):
        s = cc % 2
        kb.ld(u32[s], E.Z[(NQKV + cc) * 128:(NQKV + cc + 1) * 128, :], bu32[s])
        kb.cp("pool", ubf[s], u32[s], [bu32[s]], [bubf[s]])
        for ri in range(2):
            kb.ld(rows[ri], E.BBT[ri][cc * 128:(cc + 1) * 128, :], b_rows)
            kb.ld(crow[ri], E.c_ssm[ri][cc * 128:(cc + 1) * 128, :], b_crow)
        for ri in range(2):
            kb.tt("pool", BD[ri], bc_mid(rows[ri], 8), bc_last(bmask, 64), ALU.mult, [b_rows, b_msk], [b_BD])
            kb.cp("dve", cdup, bc_mid(crow[ri], 2), [b_crow], [b_cdup])
            kb.tr(psum[5][:, 0:128], cdup.rearrange("p a b -> p (a b)"), E.ident, [b_cdup, E.b_ident], [pb[5]])
            if ri == 0:
                kb.tt("dve", CL[ri], bc_mid(psum[5][:, 0:128], 4), cmask, ALU.mult, [pb[5], b_msk], [b_CL])
            else:
                kb.stt("dve", CL[ri], bc_mid(psum[5][:, 0:128], 4), -1.0, cmask, ALU.mult, ALU.mult, [pb[5], b_msk], [b_CL])
        for j in range(4):
            pt = cc * 4 + j
            lre = BD[0][:, 2 * j:2 * j + 2, :].rearrange("p a b -> p (a b)")
            lim = BD[1][:, 2 * j:2 * j + 2, :].rearrange("p a b -> p (a b)")
            kb.ts("pool", angT, trow[:, 0, :], th[:, pt:pt + 1], None, ALU.mult, None, [b_trow, bsp, b_trig], [b_tscr])
            kb.stt("pool", angT, trow[:, 1, :], a1[:, pt:pt + 1], angT, ALU.mult, ALU.add, [b_trow, bsp, b_tscr], [b_tscr])
            kb.ts("pool", kiT, angT, 1.0 / TWO_PI, None, ALU.mult, None, [b_tscr], [b_tscr])
            kb.cp("pool", kfT, kiT, [b_tscr], [b_tscr])
            kb.stt("pool", angT, kfT, -C1_2PI, angT, ALU.mult, ALU.add, [b_tscr], [b_tscr])
            kb.stt("pool", angT, kfT, -C2_2PI, angT, ALU.mult, ALU.add, [b_tscr], [b_tscr])
            kb.ts("pool", angT, angT, -PI_SAFE, PI_SAFE, ALU.max, ALU.min, [b_tscr], [b_tscr])
            kb.act(sinT, angT, AF.Sin, [b_tscr] + bhbf, [b_trig])
            kb.act(cosT, angT, AF.Sin, [b_tscr], [b_trig], scale=0.5)
            kb.tt("pool", cosT, cosT, cosT, ALU.mult, [b_trig], [b_trig])
            kb.ts("pool", cosT, cosT, -2.0, 1.0, ALU.mult, ALU.add, [b_trig], [b_trig])
            for tg in range(T // 512):
                sl = slice(tg * 512, (tg + 1) * 512)
                b_re = 2 * (tg % 2); b_im = b_re + 1
                kb.mm(psum[b_re][:, :], lre, ubf[s][:, sl], True, True, [b_BD, bubf[s]], [pb[b_re]])
                kb.mm(psum[b_im][:, :], lim, ubf[s][:, sl], True, True, [b_BD, bubf[s]], [pb[b_im]])
                kb.tt("dve", tmpa, psum[b_re][:, :], cosT[:, sl], ALU.mult, [pb[b_re], b_trig], [b_tmpa])
                kb.tt("dve", bt[0][:, sl], psum[b_im][:, :], sinT[:, sl], ALU.mult, [pb[b_im], b_trig, b_gg[0]], [b_bt[0]])
                kb.tt("pool", bt[0][:, sl], bt[0][:, sl], tmpa, ALU.add, [b_bt[0], b_tmpa], [b_bt[0]])
                kb.tt("dve", tmpb, psum[b_re][:, :], sinT[:, sl], ALU.mult, [pb[b_re], b_trig], [b_tmpb])
                kb.tt("dve", bt[1][:, sl], psum[b_im][:, :], cosT[:, sl], ALU.mult, [pb[b_im], b_trig, b_gg[1]], [b_bt[1]])
                kb.tt("pool", bt[1][:, sl], bt[1][:, sl], tmpb, ALU.subtract, [b_bt[1], b_tmpb], [b_bt[1]])
            rb = bcast_free(mag[:, pt:pt + 1], T)
            for q in range(2):
                kb.A("dve", (lambda o, d0, d1: (lambda e: e.tensor_tensor_scan(out=o, data0=d0, data1=d1, initial=0.0,
                                                                             op0=ALU.mult, op1=ALU.add)))(gg[q], rb, bt[q]),
                     [b_bt[q], bsp], [b_gg[q]])
            kb.tt("pool", bt[0], gg[0], cosT, ALU.mult, [b_gg[0], b_trig], [b_bt[0]])
            kb.tt("dve", bt[1], gg[1], sinT, ALU.mult, [b_gg[1], b_trig], [b_bt[1]])
            kb.tt("pool", hbf[j][0], bt[0], bt[1], ALU.subtract, [b_bt[0], b_bt[1]], [bhbf[j]])
            kb.tt("dve", hout_p[:, pt, 0:1], bt[0][:, T - 1:T], bt[1][:, T - 1:T], ALU.subtract, [b_bt[0], b_bt[1]], [b_houtp])
            kb.tt("pool", bt[0], gg[0], sinT, ALU.mult, [b_gg[0], b_trig], [b_bt[0]])
            kb.tt("dve", bt[1], gg[1], cosT, ALU.mult, [b_gg[1], b_trig], [b_bt[1]])
            kb.tt("pool", hbf[j][1], bt[0], bt[1], ALU.add, [b_bt[0], b_bt[1]], [bhbf[j]])
            kb.tt("dve", hout_p[:, pt, 1:2], bt[0][:, T - 1:T], bt[1][:, T - 1:T], ALU.add, [b_bt[0], b_bt[1]], [b_houtp])
            kb.mm(psum[6][:, 0:1], lre, us_bf[:, cc:cc + 1], True, True, [b_BD, b_usbf], [pb[6]])
            kb.mm(psum[6][:, 1:2], lim, us_bf[:, cc:cc + 1], True, True, [b_BD, b_usbf], [pb[6]])
            abre = ps_["abre"][:, pt:pt + 1]; abim = ps_["abim"][:, pt:pt + 1]
            kb.stt("dve", hout_s[:, pt, 0:1], h0[:, pt, 0:1], abre, psum[6][:, 0:1], ALU.mult, ALU.add, [b_h0, bsp, pb[6]], [b_houts])
            kb.stt("dve", hout_s[:, pt, 1:2], h0[:, pt, 1:2], abre, psum[6][:, 1:2], ALU.mult, ALU.add, [b_h0, bsp, pb[6]], [b_houts])
            kb.ts("dve", ys_q[:, 0:1], h0[:, pt, 1:2], abim, None, ALU.mult, None, [b_h0, bsp], [b_yst])
            kb.tt("dve", hout_s[:, pt, 0:1], hout_s[:, pt, 0:1], ys_q[:, 0:1], ALU.subtract, [b_yst, b_houts], [b_houts])
            kb.stt("dve", hout_s[:, pt, 1:2], h0[:, pt, 0:1], abim, hout_s[:, pt, 1:2], ALU.mult, ALU.add, [b_h0, bsp, b_houts], [b_houts])
            kb.cp("dve", hs_bf[:, pt, :], hout_s[:, pt, :], [b_houts], [b_hsbf])
        for tg in range(T // 512):
            sl = slice(tg * 512, (tg + 1) * 512)
            bk = tg % 2
            n = 0
            for j in range(4):
                for ri in range(2):
                    kb.mm(psum[bk][:, :], CL[ri][:, j, :], hbf[j][ri][:, sl], n == 0, n == 7, [b_CL, bhbf[j]], [pb[bk]])
                    n += 1
            kb.stt("dve", yv[:, sl], u32[s][:, sl], E.dssm_sb[:, cc:cc + 1], psum[bk][:, :], ALU.mult, ALU.add,
                   [bu32[s], E.b_dssm, pb[bk], b_ygbf], [b_yv])
        kb.act(ysq, yv, AF.Square, [b_yv], [b_ysq])
        kb.ts("pool", ysq, ysq, 0.044715, 1.0, ALU.mult, ALU.add, [b_ysq], [b_ysq])
        kb.tt("pool", ysq, ysq, yv, ALU.mult, [b_ysq, b_yv], [b_ysq])
        kb.act(ysq, ysq, AF.Sigmoid, [b_ysq], [b_ysq], scale=GC)
        kb.tt("pool", ygbf, yv, ysq, ALU.mult, [b_yv, b_ysq], [b_ygbf])
        kb.stq(E.YG[cc * 128:(cc + 1) * 128, :], ygbf, b_ygbf)
        n = 0
        for j in range(4):
            for ri in range(2):
                kb.mm(psum[7][:, cc:cc + 1], CL[ri][:, j, :], hs_bf[:, cc * 4 + j, ri:ri + 1], n == 0, n == 7, [b_CL, b_hsbf], [pb[7]])
                n += 1
    ygs = E.ygs
    kb.tt("dve", ys_t, E.zs[:, NQKV:NQKV + NCB], E.dssm_sb, ALU.mult, [E.b_zs, E.b_dssm], [b_yst])
    kb.tt("dve", ys_t, ys_t, psum[7][:, 0:NCB], ALU.add, [b_yst, pb[7]], [b_yst])
    kb.tt("dve", ys_q, ys_t, ys_t, ALU.mult, [b_yst], [b_yst])
    kb.ts("dve", ys_q, ys_q, 0.044715, 1.0, ALU.mult, ALU.add, [b_yst], [b_yst])
    kb.tt("dve", ys_q, ys_q, ys_t, ALU.mult, [b_yst], [b_yst])
    kb.act(ys_q, ys_q, AF.Sigmoid, [b_yst], [b_yst], scale=GC)
    kb.tt("dve", ygs, ys_t, ys_q, ALU.mult, [b_yst], [E.b_ygs])
    kb.cp("dve", E.hns[:, 0:NCB], ygs, [E.b_ygs], [E.b_hns])
    kb.stq(E.ossm_p, hout_p.rearrange("p n k -> p (n k)"), b_houtp)
    kb.stq(E.ossm_s, hout_s.rearrange("p n k -> p (n k)"), b_houts)
    P.barrier()


def make_evac_res(E, Xsrc, Xdst, tb, bias_sb=None, bias_buf=None):
    kb, P, Ar = E.kb, E.P, E.Ar
    xr = [Ar.alloc([512]) for _ in range(4)]
    bxr = [P.buf("xres%d" % i) for i in range(4)]
    st_ = {"i": 0}

    def evac(si, tgi, pss, pbs):
        m = si
        s = st_["i"] % 4
        st_["i"] += 1
        t0 = tb["t"] + tgi * 512
        kb.ld(xr[s], Xsrc[m * 128:(m + 1) * 128, t0:t0 + 512], bxr[s])
        if bias_sb is None:
            kb.tt("dve", xr[s], xr[s], pss[0], ALU.add, [bxr[s], pbs[0]], [bxr[s]])
        else:
            kb.stt("dve", xr[s], pss[0], bias_sb[:, m:m + 1], xr[s], ALU.add, ALU.add, [bxr[s], pbs[0], bias_buf], [bxr[s]])
        kb.stq(Xdst[m * 128:(m + 1) * 128, t0:t0 + 512], xr[s], bxr[s])
    return evac


def phase_glu_out(E):
    kb, P, Ar, psum, pb, c = E.kb, E.P, E.Ar, E.psum, E.pb, E.kb.c
    T, NCH, AH, NCB = c["T"], c["NCH"], c["AH"], c["NCB"]
    Ar.reset()
    IN = Ar.alloc([NCH, 1024], BF16); b_in = P.buf("in")
    wsl, wbf = E.wslots_alloc(3, NCH * 128)
    sig = [Ar.alloc([512]) for _ in range(2)]; bsig = [P.buf("sig") for _ in range(2)]
    ost = [Ar.alloc([1024], BF16) for _ in range(2)]; bost = [P.buf("ost") for _ in range(2)]
    sgs = Ar.alloc([NCB]); b_sgs = P.buf("sgs")
    for sbi in range(T // 1024):
        E.load_in(E.YG, NCB, sbi * 1024, 1024, IN, b_in)

        def evac(si, tgi, pss, pbs, sbi=sbi):
            m = si
            s = (m * 2 + tgi) % 2
            o = m % 2
            kb.act(sig[s], pss[0], AF.Sigmoid, [pbs[0], E.b_bglu], [bsig[s]], bias=E.bglu_sb[:, m:m + 1])
            kb.tt("dve", ost[o][:, tgi * 512:(tgi + 1) * 512], IN[:, m, tgi * 512:(tgi + 1) * 512], sig[s], ALU.mult,
                  [b_in, bsig[s]], [bost[o]])
            if tgi == 1:
                kb.stq(E.CAT[(AH + m) * 128:(AH + m + 1) * 128, sbi * 1024:(sbi + 1) * 1024], ost[o], bost[o])
        E.linear(lambda m: E.w_glu_t[m], [[m] for m in range(NCB)], NCB, IN, b_in, 1024, evac, wsl, wbf,
                 sample_cols=[[m] for m in range(NCB)] if sbi == 0 else None)
        if sbi == 0:
            kb.tt("dve", sgs, psum[6][:, 0:NCB], E.bglu_sb, ALU.add, [pb[6], E.b_bglu], [b_sgs])
            kb.act(sgs, sgs, AF.Sigmoid, [b_sgs], [b_sgs])
            kb.tt("dve", E.cats[:, AH:AH + NCB], E.ygs, sgs, ALU.mult, [E.b_ygs, b_sgs], [E.b_cats])
    P.barrier()
    Ar.reset()
    IN = Ar.alloc([NCH, 1024], BF16); b_in = P.buf("in")
    wsl, wbf = E.wslots_alloc(3, NCH * 128)
    kb.cp("dve", E.hns[:, 0:NCH], E.cats, [E.b_cats], [E.b_hns])
    tbh = {"t": 0}
    evac = make_evac_res(E, E.xT, E.XB, tbh)
    for sbi in range(T // 1024):
        E.load_in(E.CAT, NCH, sbi * 1024, 1024, IN, b_in)
        tbh["t"] = sbi * 1024
        E.linear(lambda m: E.w_out_t[m], [[m] for m in range(NCH)], NCH, IN, b_in, 1024, evac, wsl, wbf,
                 sample_cols=[[m] for m in range(NCH)] if sbi == 0 else None)
        if sbi == 0:
            kb.tt("dve", E.xs_sb, E.xs_sb, psum[6][:, 0:NCH], ALU.add, [pb[6], E.b_xs], [E.b_xs])
    P.barrier()


def phase_ffn(E, l, Xsrc, Xdst):
    kb, P, Ar, psum, pb, c = E.kb, E.P, E.Ar, E.psum, E.pb, E.kb.c
    T, NCH, NFF = c["T"], c["NCH"], c["NFF"]
    Ar.reset()
    IN = Ar.alloc([NCH, 1024], BF16); b_in = P.buf("in")
    wsl, wbf = E.wslots_alloc(3, 2 * NCH * 128)
    zt = [[Ar.alloc([514]) for _ in range(2)] for _ in range(2)]
    bzt = [[P.buf("zt") for _ in range(2)] for _ in range(2)]
    zc = [[Ar.alloc([512]) for _ in range(2)] for _ in range(2)]
    bzc = [[P.buf("zc") for _ in range(2)] for _ in range(2)]
    ast = [Ar.alloc([1024], BF16) for _ in range(2)]; bast = [P.buf("ast") for _ in range(2)]
    keep = Ar.off
    fdw, fb = E.fdw_sb, E.fb_sb
    kb.memset("dve", E.zprev, 0.0, [E.b_zprev])
    it = {"i": 0}
    for sbi in range(T // 1024):
        if sbi > 0:
            P.barrier()
        Ar.off = keep
        E.norm_to_in(Xsrc, E.gffn_sb, E.b_gffn, l * NCH, sbi, IN, b_in)
        if sbi == 0:
            E.norm_sample(E.gffn_sb, E.b_gffn, l * NCH)

        def evac(si, tgi, pss, pbs, sbi=sbi):
            j = si
            r = it["i"] % 2
            it["i"] += 1
            for half in range(2):
                m = j + half * NFF
                z = zt[r][half]; bz = bzt[r][half]
                eng = "dve" if half == 0 else "pool"
                kb.cp("dve", z[:, 0:2], E.zprev[:, m, :], [E.b_zprev], [bz])
                kb.cp("act", z[:, 2:514], pss[half], [pbs[half]], [bz])
                kb.cp("dve", E.zprev[:, m, :], z[:, 512:514], [bz], [E.b_zprev])
                o = zc[r][half]; bo = bzc[r][half]
                kb.ts(eng, o, z[:, 0:512], fdw[:, l, m, 0:1], fb[:, l, m:m + 1], ALU.mult, ALU.add, [bz, E.b_fdw, E.b_fb], [bo])
                kb.stt(eng, o, z[:, 1:513], fdw[:, l, m, 1:2], o, ALU.mult, ALU.add, [bz, E.b_fdw, bo], [bo])
                kb.stt(eng, o, z[:, 2:514], fdw[:, l, m, 2:3], o, ALU.mult, ALU.add, [bz, E.b_fdw, bo], [bo])
            a = j % 2
            kb.act(zc[r][0], zc[r][0], AF.Silu, [bzc[r][0]], [bzc[r][0]])
            kb.tt("pool", ast[a][:, tgi * 512:(tgi + 1) * 512], zc[r][0], zc[r][1], ALU.mult, [bzc[r][0], bzc[r][1]], [bast[a]])
            if tgi == 1:
                kb.stq(E.ACTS[j * 128:(j + 1) * 128, sbi * 1024:(sbi + 1) * 1024], ast[a], bast[a])
        E.linear(lambda m: E.w_up_t[l][m], [[j, NFF + j] for j in range(NFF)], NCH, IN, b_in, 1024, evac, wsl, wbf,
                 sample_cols=[[j, NFF + j] for j in range(NFF)] if sbi == 0 else None)
        if sbi == 0:
            zs_ = Ar.alloc([2 * NFF]); zc_ = Ar.alloc([2 * NFF]); b_s = P.buf("ffs")
            so = Ar.alloc([2 * NFF, 2]); b_so = P.buf("ffso")
            stf = Ar.alloc([2, 2 * NFF, 2]); b_stf = P.buf("stf")
            kb.ld(stf, E.st_ffn.rearrange("p (l m k) -> p l m k", l=2, k=2), b_stf)
            kb.cp("act", zs_, psum[6][:, 0:2 * NFF], [pb[6]], [b_s])
            kb.tt("dve", zc_, zs_, fdw[:, l, :, 2], ALU.mult, [b_s, E.b_fdw], [b_s])
            kb.tt("dve", so[:, :, 0], stf[:, l, :, 1], fdw[:, l, :, 1], ALU.mult, [b_stf, E.b_fdw], [b_so])
            kb.tt("dve", zc_, zc_, so[:, :, 0], ALU.add, [b_s, b_so], [b_s])
            kb.tt("dve", so[:, :, 0], stf[:, l, :, 0], fdw[:, l, :, 0], ALU.mult, [b_stf, E.b_fdw], [b_so])
            kb.tt("dve", zc_, zc_, so[:, :, 0], ALU.add, [b_s, b_so], [b_s])
            kb.tt("dve", zc_, zc_, fb[:, l, :], ALU.add, [b_s, E.b_fb], [b_s])
            kb.act(zc_[:, 0:NFF], zc_[:, 0:NFF], AF.Silu, [b_s], [b_s])
            kb.tt("dve", E.hns[:, 0:NFF], zc_[:, 0:NFF], zc_[:, NFF:2 * NFF], ALU.mult, [b_s], [E.b_hns])
            kb.cp("dve", so[:, :, 0], stf[:, l, :, 1], [b_stf, b_s], [b_so])
            kb.cp("dve", so[:, :, 1], zs_, [b_s], [b_so])
            kb.stq(E.offn_s.rearrange("p (l x) -> p l x", l=2)[:, l, :], so.rearrange("p m k -> p (m k)"), b_so)
    kb.stq(E.offn_p.rearrange("p (l x) -> p l x", l=2)[:, l, :], E.zprev.rearrange("p m k -> p (m k)"), E.b_zprev)
    P.barrier()
    Ar.reset()
    IN2 = Ar.alloc([NFF, 512], BF16); b_in2 = P.buf("in2")
    wsl, wbf = E.wslots_alloc(3, NFF * 128)
    tbh = {"t": 0}
    evac = make_evac_res(E, Xsrc, Xdst, tbh)
    for tb in range(T // 512):
        E.load_in(E.ACTS, NFF, tb * 512, 512, IN2, b_in2)
        tbh["t"] = tb * 512
        E.linear(lambda m: E.w_down_t[l][m], [[m] for m in range(NCH)], NFF, IN2, b_in2, 512, evac, wsl, wbf,
                 sample_cols=[[m] for m in range(NCH)] if tb == 0 else None)
        if tb == 0:
            kb.tt("dve", E.xs_sb, E.xs_sb, psum[6][:, 0:NCH], ALU.add, [pb[6], E.b_xs], [E.b_xs])
    P.barrier()


def phase_conv(E):
    kb, P, Ar, psum, pb, c = E.kb, E.P, E.Ar, E.psum, E.pb, E.kb.c
    T, NCH, DM = c["T"], c["NCH"], c["DM"]
    XA, XB = E.XA, E.XB
    Ar.reset()
    IN = Ar.alloc([NCH, 1024], BF16); b_in = P.buf("in")
    wsl, wbf = E.wslots_alloc(3, 2 * NCH * 128)
    sg = [Ar.alloc([512]) for _ in range(2)]; bsg = [P.buf("sg") for _ in range(2)]
    ust = [Ar.alloc([1024]) for _ in range(2)]; bust = [P.buf("ust") for _ in range(2)]
    zpad = Ar.alloc([NCH, 32]); b_zpad = P.buf("zpad")
    kb.memset("dve", zpad, 0.0, [b_zpad])
    kb.stq(E.US.rearrange("(c p) t -> p c t", p=128)[:, :, 0:32], zpad, b_zpad)
    keep = Ar.off
    bp = E.bpw1_sb
    for sbi in range(T // 1024):
        if sbi > 0:
            P.barrier()
        Ar.off = keep
        E.norm_to_in(XA, E.gmix_sb, E.b_gmix, NCH, sbi, IN, b_in)
        if sbi == 0:
            E.norm_sample(E.gmix_sb, E.b_gmix, NCH)

        def evac(si, tgi, pss, pbs, sbi=sbi):
            j = si
            s = (2 * j + tgi) % 2
            o = j % 2
            kb.act(sg[s], pss[1], AF.Sigmoid, [pbs[1], E.b_bpw1], [bsg[s]], bias=bp[:, NCH + j:NCH + j + 1])
            kb.stt("dve", ust[o][:, tgi * 512:(tgi + 1) * 512], pss[0], bp[:, j:j + 1], sg[s], ALU.add, ALU.mult,
                   [pbs[0], E.b_bpw1, bsg[s]], [bust[o]])
            if tgi == 1:
                kb.stq(E.US[j * 128:(j + 1) * 128, 32 + sbi * 1024:32 + (sbi + 1) * 1024], ust[o], bust[o])
                if sbi == T // 1024 - 1:
                    kb.stq(E.oconv_p[:, j * 30:(j + 1) * 30], ust[o][:, 1024 - 30:1024], bust[o])
        E.linear(lambda m: E.w_pw1_t[m], [[j, NCH + j] for j in range(NCH)], NCH, IN, b_in, 1024, evac, wsl, wbf,
                 sample_cols=[[j, NCH + j] for j in range(NCH)] if sbi == 0 else None)
        if sbi == 0:
            ext = Ar.alloc([NCH, 31]); b_ext = P.buf("ext"); t1 = Ar.alloc([2 * NCH]); b_t1 = P.buf("cst1")
            prod = Ar.alloc([NCH, 31]); cvs = Ar.alloc([NCH]); b_cvs = P.buf("cvs")
            st2 = Ar.alloc([4]); b_st2 = P.buf("st2")
            kb.ld(ext[:, :, 0:30], E.st_conv.rearrange("p (c k) -> p c k", k=30), b_ext)
            kb.tt("dve", t1, psum[6][:, 0:2 * NCH], bp, ALU.add, [pb[6], E.b_bpw1], [b_t1])
            kb.act(t1[:, NCH:2 * NCH], t1[:, NCH:2 * NCH], AF.Sigmoid, [b_t1], [b_t1])
            kb.tt("dve", ext[:, :, 30], t1[:, 0:NCH], t1[:, NCH:2 * NCH], ALU.mult, [b_t1], [b_ext])
            kb.stq(E.oconv_s.rearrange("p (c k) -> p c k", k=30), ext[:, :, 1:31], b_ext)
            kb.tt("dve", prod, ext, E.wdw_sb, ALU.mult, [b_ext, E.b_wdw], [b_cvs])
            kb.A("dve", lambda e: e.reduce_sum(out=cvs, in_=prod, axis=AX.X), [b_cvs], [b_cvs])
            kb.tt("dve", cvs, cvs, E.bdw_sb, ALU.add, [b_cvs, E.b_bdw], [b_cvs])
            kb.A("dve", lambda e: e.reduce_sum(out=st2[:, 0:1], in_=cvs, axis=AX.X), [b_cvs], [b_st2])
            kb.tt("dve", prod[:, :, 0], cvs, cvs, ALU.mult, [b_cvs], [b_cvs])
            kb.A("dve", lambda e: e.reduce_sum(out=st2[:, 1:2], in_=prod[:, :, 0], axis=AX.X), [b_cvs], [b_st2])
            kb.mm(psum[7][:, 300:302], E.ones_f, st2[:, 0:2], True, True, [E.b_ones, b_st2], [pb[7]])
            kb.cp("dve", st2[:, 0:2], psum[7][:, 300:302], [pb[7]], [b_st2])
            kb.tt("dve", st2[:, 2:3], st2[:, 0:1], st2[:, 0:1], ALU.mult, [b_st2], [b_st2])
            kb.tt("dve", st2[:, 2:3], st2[:, 1:2], st2[:, 2:3], ALU.subtract, [b_st2], [b_st2])
            kb.act(st2[:, 3:4], st2[:, 2:3], AF.Sqrt, [b_st2, E.b_eps], [b_st2], bias=E.eps_rms[:, 1:2])
            kb.A("dve", lambda e: e.reciprocal(out=st2[:, 2:3], in_=st2[:, 3:4]), [b_st2], [b_st2])
            kb.ts("dve", cvs, cvs, st2[:, 0:1], st2[:, 2:3], ALU.subtract, ALU.mult, [b_cvs, b_st2], [b_cvs])
            kb.tt("dve", cvs, cvs, E.lng_sb, ALU.mult, [b_cvs, E.b_lng], [b_cvs])
            kb.tt("dve", cvs, cvs, E.lnb_sb, ALU.add, [b_cvs, E.b_lnb], [b_cvs])
            kb.act(t1[:, 0:NCH], cvs, AF.Sigmoid, [b_cvs], [b_t1])
            kb.tt("dve", E.hns[:, 0:NCH], cvs, t1[:, 0:NCH], ALU.mult, [b_cvs, b_t1], [E.b_hns])
    P.barrier()
    for half in range(T // 1024):
        Ar.reset()
        ut = [Ar.alloc([1024 + 32]) for _ in range(2)]; but = [P.buf("ut") for _ in range(2)]
        accA = [Ar.alloc([1024]) for _ in range(2)]; baccA = [P.buf("accA") for _ in range(2)]
        accB = [Ar.alloc([1024]) for _ in range(2)]; baccB = [P.buf("accB") for _ in range(2)]
        sq = [Ar.alloc([1024]) for _ in range(2)]; bsq = [P.buf("csq") for _ in range(2)]
        for ch in range(NCH):
            s = ch % 2
            kb.ld(ut[s], E.US[ch * 128:(ch + 1) * 128, half * 1024:half * 1024 + 1056], but[s])
            u = ut[s]
            w = E.wdw_sb
            kb.ts("dve", accA[s], u[:, 2:2 + 1024], w[:, ch, 0:1], E.bdw_sb[:, ch:ch + 1], ALU.mult, ALU.add,
                  [but[s], E.b_wdw, E.b_bdw], [baccA[s]])
            kb.ts("pool", accB[s], u[:, 3:3 + 1024], w[:, ch, 1:2], None, ALU.mult, None, [but[s], E.b_wdw], [baccB[s]])
            for j in range(2, 31):
                if j % 2 == 0:
                    kb.stt("dve", accA[s], u[:, 2 + j:2 + j + 1024], w[:, ch, j:j + 1], accA[s], ALU.mult, ALU.add,
                           [but[s], E.b_wdw, baccA[s]], [baccA[s]])
                else:
                    kb.stt("pool", accB[s], u[:, 2 + j:2 + j + 1024], w[:, ch, j:j + 1], accB[s], ALU.mult, ALU.add,
                           [but[s], E.b_wdw, baccB[s]], [baccB[s]])
            kb.tt("dve", accA[s], accA[s], accB[s], ALU.add, [baccA[s], baccB[s]], [baccA[s]])
            kb.stq(E.YC[ch * 128:(ch + 1) * 128, half * 1024:(half + 1) * 1024], accA[s], baccA[s])
            kb.act(sq[s], accA[s], AF.Square, [baccA[s]], [bsq[s]])
            for tg in range(2):
                sl = slice(tg * 512, (tg + 1) * 512)
                kb.mm(psum[tg][:, :], E.ones_f, accA[s][:, sl], ch == 0, ch == NCH - 1, [E.b_ones, baccA[s]], [pb[tg]])
                kb.mm(psum[2 + tg][:, :], E.ones_f, sq[s][:, sl], ch == 0, ch == NCH - 1, [E.b_ones, bsq[s]], [pb[2 + tg]])
        mean = Ar.alloc([1024]); rstd = Ar.alloc([1024]); b_st = P.buf("lnst"); tmp = Ar.alloc([1024]); b_tmp = P.buf("lntmp")
        for tg in range(2):
            sl = slice(tg * 512, (tg + 1) * 512)
            kb.cp("act", mean[:, sl], psum[tg][:, :], [pb[tg]], [b_st])
            kb.tt("dve", tmp[:, sl], mean[:, sl], mean[:, sl], ALU.mult, [b_st], [b_tmp])
            kb.tt("dve", tmp[:, sl], psum[2 + tg][:, :], tmp[:, sl], ALU.subtract, [pb[2 + tg], b_tmp], [b_tmp])
            kb.act(tmp[:, sl], tmp[:, sl], AF.Sqrt, [b_tmp, E.b_eps], [b_tmp], bias=E.eps_rms[:, 1:2])
            kb.A("dve", (lambda o, i_: (lambda e: e.reciprocal(out=o, in_=i_)))(rstd[:, sl], tmp[:, sl]), [b_tmp], [b_st])
        P.barrier()
        yt = [Ar.alloc([1024]) for _ in range(2)]; byt = [P.buf("yt") for _ in range(2)]
        sgm = [Ar.alloc([1024]) for _ in range(2)]; bsgm = [P.buf("sgm") for _ in range(2)]
        swb = [Ar.alloc([1024], BF16) for _ in range(2)]; bswb = [P.buf("swb") for _ in range(2)]
        for ch in range(NCH):
            s = ch % 2
            kb.ld(yt[s], E.YC[ch * 128:(ch + 1) * 128, half * 1024:(half + 1) * 1024], byt[s])
            kb.tt("dve", yt[s], yt[s], mean, ALU.subtract, [byt[s], b_st], [byt[s]])
            kb.tt("pool", yt[s], yt[s], rstd, ALU.mult, [byt[s], b_st], [byt[s]])
            kb.ts("dve", yt[s], yt[s], E.lng_sb[:, ch:ch + 1], E.lnb_sb[:, ch:ch + 1], ALU.mult, ALU.add,
                  [byt[s], E.b_lng, E.b_lnb], [byt[s]])
            kb.act(sgm[s], yt[s], AF.Sigmoid, [byt[s]], [bsgm[s]])
            kb.tt("pool", swb[s], yt[s], sgm[s], ALU.mult, [byt[s], bsgm[s]], [bswb[s]])
            kb.stq(E.SW[ch * 128:(ch + 1) * 128, half * 1024:(half + 1) * 1024], swb[s], bswb[s])
        P.barrier()
    Ar.reset()
    IN = Ar.alloc([NCH, 1024], BF16); b_in = P.buf("in")
    wsl, wbf = E.wslots_alloc(3, NCH * 128)
    tbh = {"t": 0}
    evac = make_evac_res(E, XA, XB, tbh, E.bpw2_sb, E.b_bpw2)
    for sbi in range(T // 1024):
        E.load_in(E.SW, NCH, sbi * 1024, 1024, IN, b_in)
        tbh["t"] = sbi * 1024
        E.linear(lambda m: E.w_pw2_t[m], [[m] for m in range(NCH)], NCH, IN, b_in, 1024, evac, wsl, wbf,
                 sample_cols=[[m] for m in range(NCH)] if sbi == 0 else None)
        if sbi == 0:
            kb.tt("dve", E.xs_sb, E.xs_sb, psum[6][:, 0:NCH], ALU.add, [pb[6], E.b_xs], [E.b_xs])
            kb.tt("dve", E.xs_sb, E.xs_sb, E.bpw2_sb, ALU.add, [E.b_bpw2, E.b_xs], [E.b_xs])
    P.barrier()


def phase_final(E):
    kb, P, Ar, psum, pb, c = E.kb, E.P, E.Ar, E.psum, E.pb, E.kb.c
    T, NCH = c["T"], c["NCH"]
    Ar.reset()
    yo = [Ar.alloc([256]) for _ in range(4)]; byo = [P.buf("yo") for _ in range(4)]
    keep = Ar.off
    for blk in range(T // 1024):
        if blk > 0:
            P.barrier()
        Ar.off = keep

        def emit(cc_, tgi, xin, bx, rs, brs, blk=blk):
            s = cc_ % 4
            t0 = blk * 1024 + tgi * 256
            kb.stt("dve" if cc_ % 2 == 0 else "pool", yo[s], xin, E.gfin_sb[:, cc_:cc_ + 1], rs, ALU.mult, ALU.mult,
                   [bx, E.b_gfin, brs], [byo[s]])
            kb.stq(E.yT[cc_ * 128:(cc_ + 1) * 128, t0:t0 + 256], yo[s], byo[s])
        E.norm_tokens(E.XA, E.gfin_sb, E.b_gfin, 0, blk * 1024, 1024, emit)
    t1 = Ar.alloc([NCH]); bt1 = P.buf("fs1"); t2 = Ar.alloc([2]); bt2 = P.buf("fs2"); yso = Ar.alloc([NCH]); byso = P.buf("yso")
    kb.tt("dve", t1, E.xs_sb, E.xs_sb, ALU.mult, [E.b_xs], [bt1])
    kb.A("dve", lambda e: e.reduce_sum(out=t2[:, 0:1], in_=t1, axis=AX.X), [bt1], [bt2])
    kb.mm(psum[7][:, 256:257], E.ones_f, t2[:, 0:1], True, True, [E.b_ones, bt2], [pb[7]])
    kb.act(t2[:, 1:2], psum[7][:, 256:257], AF.Sqrt, [pb[7], E.b_eps], [bt2], bias=E.eps_rms[:, 0:1])
    kb.A("dve", lambda e: e.reciprocal(out=t2[:, 0:1], in_=t2[:, 1:2]), [bt2], [bt2])
    kb.stt("dve", yso, E.xs_sb, t2[:, 0:1], E.gfin_sb, ALU.mult, ALU.mult, [E.b_xs, bt2, E.b_gfin], [byso])
    kb.stq(E.ys_o, yso, byso)
    P.barrier()


def _pmaj(v):
    v = np.asarray(v, np.float32).reshape(-1, 128)
    return np.ascontiguousarray(v.T)


def _tile_w(W, KC, NM):
    W = np.asarray(W, np.float32)
    return np.ascontiguousarray(W.reshape(KC, 128, NM, 128).transpose(2, 1, 0, 3).reshape(NM, 128, KC * 128))


def host_consts(c):
    T = c["T"]
    ident = np.eye(128, dtype=np.float32)
    pm = np.zeros((128, 128), np.float32)
    for m in range(64):
        pm[m + 64, m] = -1.0
        pm[m, m + 64] = 1.0
    k = np.arange(128)[:, None]; q = np.arange(128)[None, :]
    mask2 = np.concatenate([(k <= q), (k >= q)], axis=1).astype(np.float32)
    bmask = (np.arange(128)[:, None] // 16 == np.arange(8)[None, :]).astype(np.float32)
    sp = np.arange(128)[:, None, None] // 64
    j = np.arange(4)[None, :, None]
    gl = np.arange(128)[None, None, :] // 16
    cmask = (gl == 2 * j + sp).astype(np.float32).reshape(128, 512)
    t = np.arange(T)
    trow = np.broadcast_to(np.concatenate([t % 64, t // 64]).astype(np.float32)[None, :], (128, 2 * T)).copy()
    posrow = np.broadcast_to(np.concatenate([t, [c["PAST"]]]).astype(np.float32)[None, :], (128, T + 1)).copy()
    invf = (10000.0 ** (-(np.arange(128) % 64).astype(np.float32) / 64.0)).astype(np.float32).reshape(128, 1)
    return dict(ident=ident, pm=pm, mask2=mask2, bmask=bmask, cmask=cmask, trow=trow, posrow=posrow, invf=invf)


def host_inputs(c, inp):
    DM, T, NCH, AH, NCB, NPT, NMIN, NFF, SG = (c[k] for k in ("DM", "T", "NCH", "AH", "NCB", "NPT", "NMIN", "NFF", "SG"))
    f = lambda a: np.asarray(a, np.float32)
    sh = {}
    sh.update(host_consts(c))
    sh["g_mix"] = np.ascontiguousarray(f(inp["norm_mix"]).reshape(2, NCH, 128).transpose(2, 0, 1).reshape(128, 2 * NCH))
    sh["g_ffn"] = np.ascontiguousarray(f(inp["norm_ffn"]).reshape(2, NCH, 128).transpose(2, 0, 1).reshape(128, 2 * NCH))
    sh["g_fin"] = _pmaj(inp["norm_final"])
    sh["w_in_t"] = _tile_w(f(inp["w_in_even"])[0], NCH, NMIN)
    sh["w_out_t"] = _tile_w(f(inp["w_out_even"])[0], NCH, NCH)
    sh["w_glu_t"] = _tile_w(f(inp["w_glu"])[0], NCB, NCB)
    sh["b_glu"] = _pmaj(f(inp["b_glu"])[0])
    sh["w_pw1_t"] = _tile_w(f(inp["w_pw1"])[0], NCH, 2 * NCH)
    sh["b_pw1"] = _pmaj(f(inp["b_pw1"])[0])
    sh["w_dw"] = np.ascontiguousarray(f(inp["w_dw"])[0].T.reshape(NCH, 128, 31).transpose(1, 0, 2).reshape(128, NCH * 31))
    sh["b_dw"] = _pmaj(f(inp["b_dw"])[0]); sh["ln_g"] = _pmaj(f(inp["ln_g"])[0]); sh["ln_b"] = _pmaj(f(inp["ln_b"])[0])
    sh["w_pw2_t"] = _tile_w(f(inp["w_pw2"])[0], NCH, NCH)
    sh["b_pw2"] = _pmaj(f(inp["b_pw2"])[0])
    sh["w_up_t"] = np.stack([_tile_w(f(inp["w_up"])[l], NCH, 2 * NFF) for l in range(2)])
    sh["w_down_t"] = np.stack([_tile_w(f(inp["w_down"])[l], NFF, NCH) for l in range(2)])
    sh["ffn_dw"] = np.ascontiguousarray(f(inp["ffn_dw"]).reshape(2, 3, 2 * NFF, 128).transpose(3, 0, 2, 1).reshape(128, -1))
    sh["ffn_b"] = np.ascontiguousarray(f(inp["ffn_dw_b"]).reshape(2, 2 * NFF, 128).transpose(2, 0, 1).reshape(128, -1))
    are, aim = f(inp["ssm_a_re"])[0], f(inp["ssm_a_im"])[0]
    ldt = np.broadcast_to(f(inp["ssm_log_dt"])[0][:, None], (SG, 64))
    sh["a_gp"] = np.ascontiguousarray(np.concatenate([are, aim, ldt], axis=1))
    tosp = lambda a: a.reshape(NPT, 2, 64).transpose(1, 2, 0).reshape(128, NPT)
    sh["a_sp"] = np.ascontiguousarray(np.concatenate([tosp(are), tosp(aim), tosp(ldt)], axis=1))
    sh["b_ssm"] = np.ascontiguousarray(np.stack([f(inp["ssm_b_re"])[0], f(inp["ssm_b_im"])[0]], axis=1).reshape(SG, -1))
    sh["c_ssm"] = np.ascontiguousarray(np.stack([f(inp["ssm_c_re"])[0].reshape(SG * 16, 64), f(inp["ssm_c_im"])[0].reshape(SG * 16, 64)]))
    sh["d_ssm"] = _pmaj(f(inp["ssm_d"])[0].reshape(-1))
    maps = []
    xp = f(inp["x_prompt"]); xsm = f(inp["x_sample"])
    nb = xp.shape[0]
    xTs = [np.ascontiguousarray(xp[b].T) for b in range(nb)]
    for core in range(8):
        b = core % nb
        s = core % xsm.shape[0]
        m = dict(sh)
        m["xT"] = xTs[b]
        m["xs"] = _pmaj(xsm[s, 0])
        for g, cname in enumerate(("cache_a0", "cache_a1", "cache_a2")):
            ca = f(inp[cname])[0, s]
            m["cache%d" % g] = np.ascontiguousarray(ca.reshape(ca.shape[0], -1))
        m["st_ssm"] = np.ascontiguousarray(f(inp["state_ssm"])[0, s].reshape(NPT, 2, 64, 2).transpose(1, 2, 0, 3).reshape(128, NPT * 2))
        m["st_conv"] = np.ascontiguousarray(f(inp["state_conv"])[0, s].T.reshape(NCH, 128, 30).transpose(1, 0, 2).reshape(128, NCH * 30))
        m["st_ffn"] = np.ascontiguousarray(f(inp["state_ffn"])[:, s].reshape(2, 2, 2 * NFF, 128).transpose(3, 0, 2, 1).reshape(128, -1))
        maps.append(m)
    return maps


def host_outputs(c, res, nb, ns):
    DM, T, NCH, AH, NPT, NFF, SG = (c[k] for k in ("DM", "T", "NCH", "AH", "NPT", "NFF", "SG"))
    LG = (128, 512, 2048)
    y_p = np.stack([np.ascontiguousarray(res[b]["yT"].T) for b in range(nb)])
    y_s = np.stack([res[s]["ys"].T.reshape(1, DM) for s in range(ns)])
    outs = [y_p, y_s]
    for g in range(3):
        outs.append(np.stack([res[b]["oa%d" % g].reshape(LG[g], 2, AH, 128) for b in range(nb)])[None])
        outs.append(np.stack([res[s]["sa%d" % g].reshape(LG[g], 2, AH, 128) for s in range(ns)])[None])
    unsp = lambda a: a.reshape(2, 64, NPT, 2).transpose(2, 0, 1, 3).reshape(SG, 64, 2)
    outs.append(np.stack([unsp(res[b]["ossm_p"]) for b in range(nb)])[None])
    outs.append(np.stack([unsp(res[s]["ossm_s"]) for s in range(ns)])[None])
    uncv = lambda a: a.reshape(128, NCH, 30).transpose(2, 1, 0).reshape(30, DM)
    outs.append(np.stack([uncv(res[b]["oconv_p"]) for b in range(nb)])[None])
    outs.append(np.stack([uncv(res[s]["oconv_s"]) for s in range(ns)])[None])
    unff = lambda a: a.reshape(128, 2, 2 * NFF, 2).transpose(1, 3, 2, 0).reshape(2, 2, 2 * NFF * 128)
    outs.append(np.stack([unff(res[b]["offn_p"]) for b in range(nb)], axis=1))
    outs.append(np.stack([unff(res[s]["offn_s"]) for s in range(ns)], axis=1))
    return tuple(np.ascontiguousarray(o, dtype=np.float32) for o in outs)


_PROG_CACHE = {}


def run_cfg(cfg, inputs):
    key = tuple(sorted(cfg.items()))
    if key not in _PROG_CACHE:
        _PROG_CACHE[key] = build_program(cfg)
    kbd = _PROG_CACHE[key]
    c = kbd.c
    maps = host_inputs(c, inputs)
    names = set(kbd.din.keys())
    maps = [{k: v for k, v in m.items() if k in names} for m in maps]
    res = run_bass_kernel_spmd(kbd.nc, maps, core_ids=list(range(8)))
    return host_outputs(c, res.results, np.asarray(inputs["x_prompt"]).shape[0], np.asarray(inputs["x_sample"]).shape[0])


def kernel(**inputs):
    return run_cfg(FULL, inputs)
```
